# Optimizing a Trainium2 kernel written in Bass

```python
import math
import jax, jax.numpy as jnp
from jax import lax
import numpy as np

D_MODEL = 1024
BATCH = 8
SEQ = 8192
DEPTH = 1

D_MIX = D_MODEL
DIFF_WIDTH = D_MIX // 2
GLA_WIDTH = D_MIX - DIFF_WIDTH
DIFF_HEADS = 4
DIFF_V_DIM = DIFF_WIDTH // DIFF_HEADS
DIFF_QK_DIM = DIFF_V_DIM // 2
GLA_HEADS = 4
GLA_V_DIM = GLA_WIDTH // GLA_HEADS
GLA_K_DIM = GLA_V_DIM // 2
GLA_GATE_RANK = 16
GLA_GATE_NORMALIZER = 16.0
GLA_CHUNK = 64
Q_BLOCK = 128
EPS = 1e-6

DIFF_QK_COLS = DIFF_HEADS * 2 * DIFF_QK_DIM
GLA_QK_COLS = GLA_HEADS * GLA_K_DIM
IN_SIZES = (
    DIFF_QK_COLS,
    DIFF_QK_COLS,
    DIFF_WIDTH,
    DIFF_WIDTH,
    GLA_QK_COLS,
    GLA_QK_COLS,
    GLA_WIDTH,
    GLA_WIDTH,
    GLA_GATE_RANK,
)
IN_COLS = sum(IN_SIZES)

kernel_name = "hymba_diffattn_gla_hybrid"


def rms_norm(x, w):
    xf = x.astype(jnp.float32)
    y = xf * lax.rsqrt(jnp.mean(xf * xf, axis=-1, keepdims=True) + EPS)
    return (y * w.astype(jnp.float32)).astype(x.dtype)


def lambda_init(layer_idx):
    return 0.8 - 0.6 * math.exp(-0.3 * layer_idx)


def alibi_slopes(n_heads):
    return jnp.exp2(-8.0 * jnp.arange(1, n_heads + 1, dtype=jnp.float32) / n_heads)


def diff_attention(q, k, v, lam, slopes):
    B, H, _, T, d = q.shape
    nb = T // Q_BLOCK
    scale = d ** -0.5
    qf = q.astype(jnp.float32) * scale
    kf = k.astype(jnp.float32)
    vf = v.astype(jnp.float32)
    q_blocks = jnp.moveaxis(qf.reshape(B, H, 2, nb, Q_BLOCK, d), 3, 0)
    key_pos = jnp.arange(T, dtype=jnp.int32)

    def one_block(args):
        q_blk, blk = args
        q_pos = blk * Q_BLOCK + jnp.arange(Q_BLOCK, dtype=jnp.int32)
        dist = q_pos[:, None] - key_pos[None, :]
        bias = -slopes[:, None, None] * dist.astype(jnp.float32)
        s = jnp.einsum('bhiqd,bhikd->bhiqk', q_blk, kf) + bias[None, :, None]
        s = jnp.where(dist[None, None, None] >= 0, s, -jnp.inf)
        p = jax.nn.softmax(s, axis=-1)
        p_diff = p[:, :, 0] - lam * p[:, :, 1]
        return jnp.einsum('bhqk,bhkd->bhqd', p_diff, vf)

    out = lax.map(one_block, (q_blocks, jnp.arange(nb, dtype=jnp.int32)))
    return jnp.moveaxis(out, 0, 2).reshape(B, H, T, v.shape[-1]).astype(v.dtype)


def gla_chunked(q, k, v, g):
    B, H, T, dk = q.shape
    dv = v.shape[-1]
    C = GLA_CHUNK
    N = T // C
    qf = (q.astype(jnp.float32) * dk ** -0.5).reshape(B, H, N, C, dk)
    kf = k.astype(jnp.float32).reshape(B, H, N, C, dk)
    vf = v.astype(jnp.float32).reshape(B, H, N, C, dv)
    b = jnp.cumsum(g.astype(jnp.float32).reshape(B, H, N, C, dk), axis=3)
    b_last = b[:, :, :, -1:]
    q_in = qf * jnp.exp(b)
    k_in = kf * jnp.exp(-b)
    k_state = kf * jnp.exp(b_last - b)
    causal = jnp.tril(jnp.ones((C, C), dtype=bool))
    a = jnp.einsum('bhncd,bhnsd->bhncs', q_in, k_in)
    a = jnp.where(causal, a, 0.0)
    o_intra = jnp.einsum('bhncs,bhnse->bhnce', a, vf)
    kv = jnp.einsum('bhncd,bhnce->bhnde', k_state, vf)
    decay = jnp.exp(b_last[:, :, :, 0])

    def step(state, inp):
        kv_n, dec_n = inp
        return state * dec_n[..., None] + kv_n, state

    s0 = jnp.zeros((B, H, dk, dv), jnp.float32)
    _, s_prev = lax.scan(step, s0, (jnp.moveaxis(kv, 2, 0), jnp.moveaxis(decay, 2, 0)))
    s_prev = jnp.moveaxis(s_prev, 0, 2)
    o_inter = jnp.einsum('bhncd,bhnde->bhnce', q_in, s_prev)
    return (o_intra + o_inter).reshape(B, H, T, dv).astype(v.dtype)


def hybrid_layer(x, norm_w, w_in, w_gate_up, b_gate, lq1, lk1, lq2, lk2,
                 diff_subln_w, gla_norm_w, w_out, layer_idx):
    B, T, _ = x.shape
    h = rms_norm(x, norm_w)
    proj = jnp.einsum('btd,de->bte', h, w_in)
    split_points = []
    acc = 0
    for s in IN_SIZES[:-1]:
        acc += s
        split_points.append(acc)
    dq, dk, dv, dg, gq, gk, gv, gg, ga = jnp.split(proj, split_points, axis=-1)

    q_a = dq.reshape(B, T, DIFF_HEADS, 2, DIFF_QK_DIM).transpose(0, 2, 3, 1, 4)
    k_a = dk.reshape(B, T, DIFF_HEADS, 2, DIFF_QK_DIM).transpose(0, 2, 3, 1, 4)
    v_a = dv.reshape(B, T, DIFF_HEADS, DIFF_V_DIM).transpose(0, 2, 1, 3)
    lam_init = lambda_init(layer_idx)
    lam = (jnp.exp(jnp.sum(lq1.astype(jnp.float32) * lk1.astype(jnp.float32)))
           - jnp.exp(jnp.sum(lq2.astype(jnp.float32) * lk2.astype(jnp.float32)))
           + lam_init)
    o_a = diff_attention(q_a, k_a, v_a, lam, alibi_slopes(DIFF_HEADS))
    o_a = rms_norm(o_a, diff_subln_w) * (1.0 - lam_init)
    o_a = o_a.transpose(0, 2, 1, 3).reshape(B, T, DIFF_WIDTH) * jax.nn.silu(dg)

    q_b = gq.reshape(B, T, GLA_HEADS, GLA_K_DIM).transpose(0, 2, 1, 3)
    k_b = gk.reshape(B, T, GLA_HEADS, GLA_K_DIM).transpose(0, 2, 1, 3)
    v_b = gv.reshape(B, T, GLA_HEADS, GLA_V_DIM).transpose(0, 2, 1, 3)
    g_logit = jnp.einsum('btr,rk->btk', ga, w_gate_up) + b_gate
    g_log = jax.nn.log_sigmoid(g_logit.astype(jnp.float32)) / GLA_GATE_NORMALIZER
    g_b = g_log.reshape(B, T, GLA_HEADS, GLA_K_DIM).transpose(0, 2, 1, 3)
    o_b = gla_chunked(q_b, k_b, v_b, g_b)
    o_b = rms_norm(o_b, gla_norm_w)
    o_b = o_b.transpose(0, 2, 1, 3).reshape(B, T, GLA_WIDTH) * jax.nn.silu(gg)

    y = jnp.concatenate([o_a, o_b], axis=-1)
    return x + jnp.einsum('btm,md->btd', y, w_out)


def setup_inputs(seed: int = 0) -> dict:
    key = jax.random.key(seed)
    ks = jax.random.split(key, 16)
    f32 = jnp.float32
    x = jax.random.normal(ks[0], (BATCH, SEQ, D_MODEL), f32)
    norm_w = 1.0 + 0.01 * jax.random.normal(ks[1], (DEPTH, D_MODEL), f32)
    w_in = jax.random.normal(ks[2], (DEPTH, D_MODEL, IN_COLS), f32) * D_MODEL ** -0.5
    w_gate_up = jax.random.normal(ks[3], (DEPTH, GLA_GATE_RANK, GLA_QK_COLS), f32) * GLA_GATE_RANK ** -0.5
    b_gate = 0.1 * jax.random.normal(ks[4], (DEPTH, GLA_QK_COLS), f32)
    lambda_q1 = 0.1 * jax.random.normal(ks[5], (DEPTH, DIFF_QK_DIM), f32)
    lambda_k1 = 0.1 * jax.random.normal(ks[6], (DEPTH, DIFF_QK_DIM), f32)
    lambda_q2 = 0.1 * jax.random.normal(ks[7], (DEPTH, DIFF_QK_DIM), f32)
    lambda_k2 = 0.1 * jax.random.normal(ks[8], (DEPTH, DIFF_QK_DIM), f32)
    diff_subln_w = 1.0 + 0.01 * jax.random.normal(ks[9], (DEPTH, DIFF_V_DIM), f32)
    gla_norm_w = 1.0 + 0.01 * jax.random.normal(ks[10], (DEPTH, GLA_V_DIM), f32)
    w_out = jax.random.normal(ks[11], (DEPTH, D_MIX, D_MODEL), f32) * D_MIX ** -0.5
    final_norm_w = 1.0 + 0.01 * jax.random.normal(ks[12], (D_MODEL,), f32)
    return {"x": x, "norm_w": norm_w, "w_in": w_in, "w_gate_up": w_gate_up,
            "b_gate": b_gate, "lambda_q1": lambda_q1, "lambda_k1": lambda_k1,
            "lambda_q2": lambda_q2, "lambda_k2": lambda_k2,
            "diff_subln_w": diff_subln_w, "gla_norm_w": gla_norm_w,
            "w_out": w_out, "final_norm_w": final_norm_w}


def reference(x, norm_w, w_in, w_gate_up, b_gate, lambda_q1, lambda_k1, lambda_q2,
              lambda_k2, diff_subln_w, gla_norm_w, w_out, final_norm_w):
    h = x
    for layer in range(DEPTH):
        h = hybrid_layer(h, norm_w[layer], w_in[layer], w_gate_up[layer], b_gate[layer],
                         lambda_q1[layer], lambda_k1[layer], lambda_q2[layer], lambda_k2[layer],
                         diff_subln_w[layer], gla_norm_w[layer], w_out[layer], layer)
    return rms_norm(h, final_norm_w)
```

```python
import math
from contextlib import ExitStack

import numpy as np
import concourse.bass as bass
import concourse.mybir as mybir
from concourse.bass_utils import run_bass_kernel_spmd

F32 = mybir.dt.float32
BF16 = mybir.dt.bfloat16
I32 = mybir.dt.int32
AF = mybir.ActivationFunctionType
ALU = mybir.AluOpType

D = 1024
INC = 3600
C_DQ, C_DK, C_DV, C_DG, C_GQ, C_GK, C_GV, C_GG, C_GA = 0, 512, 1024, 1536, 2048, 2304, 2560, 3072, 3584
SLOPES = [2.0 ** (-8.0 * (h + 1) / 4) for h in range(4)]
LAM_INIT = 0.8 - 0.6 * math.exp(-0.3 * 0)
EPS = 1e-6
SEQ = 8192
import os
GST = int(os.environ.get('GST', '6'))
WIN_THR = 128.0
P4L = int(os.environ.get('P4L', '9'))
FUSE4 = False


class R:
    __slots__ = ("name", "w", "r")

    def __init__(self, name):
        self.name = name
        self.w = {}
        self.r = {}


class Ctx:
    def __init__(self, nc):
        self.nc = nc
        self.E = dict(pe=nc.tensor, act=nc.scalar, dve=nc.vector, pool=nc.gpsimd, sp=nc.sync)
        self.sem = {}
        self.cnt = {}
        self.seen = {e: {} for e in self.E}
        self.stack = ExitStack()
        for e in self.E:
            self.newsem(e)
        self.ndsem = 0

    def newsem(self, key):
        s = self.stack.enter_context(self.nc.semaphore("s_" + key))
        self.sem[key] = s
        self.cnt[key] = 0
        return key

    def dsem(self, i):
        key = "d%d" % i
        if key not in self.sem:
            self.newsem(key)
        return key

    def _wait(self, e, deps):
        for k, v in deps.items():
            if v <= 0:
                continue
            if self.seen[e].get(k, 0) < v:
                self.E[e].wait_ge(self.sem[k], v)
                self.seen[e][k] = v

    def _deps(self, e, reads, writes, skip=None):
        deps = {}
        for r in reads:
            for k, v in r.w.items():
                if k == e and e == "pe":
                    continue
                if k == skip:
                    continue
                if deps.get(k, 0) < v:
                    deps[k] = v
        for w in writes:
            dd = w.r if w.r else w.w
            for k, v in dd.items():
                if k == skip or (k == e and e == "pe"):
                    continue
                if deps.get(k, 0) < v:
                    deps[k] = v
        return deps

    def op(self, e, fn, reads=(), writes=(), inc=True):
        self._wait(e, self._deps(e, reads, writes))
        ins = fn(self.E[e])
        if inc:
            self.cnt[e] += 1
            ins.then_inc(self.sem[e], 1)
            c = self.cnt[e]
        else:
            c = self.cnt[e] + 1
        for r in reads:
            r.r[e] = c
        for w in writes:
            w.w = {e: c}
            w.r = {}
        return ins

    def dma(self, q, out, in_, sem, reads=(), writes=()):
        self._wait(q, self._deps(q, reads, writes, skip=sem))
        ins = self.E[q].dma_start(out=out, in_=in_)
        self.cnt[sem] += 16
        ins.then_inc(self.sem[sem], 16)
        c = self.cnt[sem]
        for r in reads:
            r.r[sem] = c
        for w in writes:
            if sem in w.w and not w.r:
                w.w[sem] = c
            else:
                w.w = {sem: c}
            w.r = {}
        return ins

    def barrier(self):
        targets = dict(self.cnt)
        for e in self.E:
            self._wait(e, {k: v for k, v in targets.items() if k != e})


def build(T=SEQ, dbg=False):
    nc = bass.Bass("TRN2", target_bir_lowering=False)
    NB = T // 512
    NT = T // 128
    kind_s = "ExternalOutput" if dbg else "Internal"

    def dram_in(name, shape):
        return nc.dram_tensor(name, shape, F32, kind="ExternalInput").ap()

    x = dram_in("x", [T, D])
    norm_w = dram_in("norm_w", [1, D])
    w_in = dram_in("w_in", [D, INC])
    w_gate_up = dram_in("w_gate_up", [16, 256])
    b_gate = dram_in("b_gate", [1, 256])
    lq1 = dram_in("lambda_q1", [1, 64])
    lk1 = dram_in("lambda_k1", [1, 64])
    lq2 = dram_in("lambda_q2", [1, 64])
    lk2 = dram_in("lambda_k2", [1, 64])
    diff_subln_w = dram_in("diff_subln_w", [1, 128])
    gla_norm_w = dram_in("gla_norm_w", [1, 128])
    w_out = dram_in("w_out", [D, D])
    final_norm_w = dram_in("final_norm_w", [1, D])
    out = nc.dram_tensor("out", [T, D], F32, kind="ExternalOutput").ap()

    def scr(name, shape, dt=BF16):
        return nc.dram_tensor(name, shape, dt, kind=kind_s).ap()

    s_qT = scr("s_qT", [4, 128, T])
    s_kT = scr("s_kT", [4, 128, T])
    s_gT = scr("s_gT", [4, 128, T])
    s_gqT = scr("s_gqT", [2, 128, T])
    s_gkT = scr("s_gkT", [2, 128, T])
    s_ggT = scr("s_ggT", [4, 128, T])
    s_gaT = scr("s_gaT", [16, T])
    s_v = scr("s_v", [T, 512])
    s_gv = scr("s_gv", [T, 512])
    s_gk = scr("s_gk", [T, 256])
    s_yT = scr("s_yT", [8, 128, T])

    cx = Ctx(nc)
    op, dma = cx.op, cx.dma

    gst = ExitStack()

    def sb(st, name, shape, dt):
        return st.enter_context(nc.sbuf_tensor(name, shape, dt))

    def pst(st, name, shape, dt=F32):
        return st.enter_context(nc.psum_tensor(name, shape, dt))

    ident = sb(gst, "ident", [128, 128], BF16)
    onesb = sb(gst, "onesb", [128, 128], BF16)
    onesf = sb(gst, "onesf", [128, 128], F32)
    onesm = sb(gst, "onesm", [128, 128], F32)
    negs = sb(gst, "negs", [128, 128], F32)
    Ucs = sb(gst, "Ucs", [128, 128], F32)
    Ust = sb(gst, "Ust", [128, 128], F32)
    gmask = sb(gst, "gmask", [128, 128], F32)
    onesmb = sb(gst, "onesmb", [128, 128], BF16)
    negbig = sb(gst, "negbig", [128, 128], BF16)
    Mtri = sb(gst, "Mtri", [128, 128], BF16)
    Ucsb = sb(gst, "Ucsb", [128, 128], BF16)
    Ustb = sb(gst, "Ustb", [128, 128], BF16)
    iot = sb(gst, "iot", [128, NT + 4], I32)
    bts = [sb(gst, "bt%d" % h, [128, NT + 4], F32) for h in range(4)]
    nw_b = sb(gst, "nw_b", [128, D], F32)
    fnw_b = sb(gst, "fnw_b", [128, D], F32)
    lam4 = sb(gst, "lam4", [128, 4, 64], F32)
    lamp = sb(gst, "lamp", [128, 2, 64], F32)
    lams = sb(gst, "lams", [128, 8], F32)
    neglam = sb(gst, "neglam", [128, 1], F32)
    wcol_d = sb(gst, "wcol_d", [128, 1], F32)
    wcol_g = sb(gst, "wcol_g", [128, 1], F32)
    epsc = sb(gst, "epsc", [128, 1], F32)
    Rc = R("consts")
    Rl = R("lam")

    P = [("pool", lambda e: e.memset(onesb[:], 1.0)),
         ("pool", lambda e: e.memset(onesf[:], 1.0)),
         ("pool", lambda e: e.memset(onesm[:], 1.0 / 128)),
         ("pool", lambda e: e.memset(negs[:], -1.0 / 16)),
         ("pool", lambda e: e.memset(epsc[:], EPS))]
    for e, f in P:
        op(e, f, writes=[Rc])
    op("pool", lambda e: e.affine_select(out=ident[:], in_=onesb[:], pattern=[[1, 128]], compare_op=ALU.is_equal,
                                          fill=0.0, base=0, channel_multiplier=-1), reads=[Rc], writes=[Rc])
    op("pool", lambda e: e.affine_select(out=Ucs[:], in_=negs[:], pattern=[[1, 128]], compare_op=ALU.is_ge,
                                          fill=0.0, base=0, channel_multiplier=-1), reads=[Rc], writes=[Rc])
    op("pool", lambda e: e.affine_select(out=Ust[:], in_=negs[:], pattern=[[-1, 128]], compare_op=ALU.is_ge,
                                          fill=0.0, base=-1, channel_multiplier=1), reads=[Rc], writes=[Rc])
    op("pool", lambda e: e.affine_select(out=gmask[:], in_=onesf[:], pattern=[[1, 128]], compare_op=ALU.is_ge,
                                          fill=0.0, base=0, channel_multiplier=-1), reads=[Rc], writes=[Rc])
    op("pool", lambda e: e.iota(iot[:], pattern=[[-128, NT + 4]], base=384, channel_multiplier=1),
       reads=[Rc], writes=[Rc])
    op("pool", lambda e: e.tensor_copy(out=onesmb[:], in_=onesm[:]), reads=[Rc], writes=[Rc])
    op("pool", lambda e: e.memset(negbig[:], -30000.0), writes=[Rc])
    op("pool", lambda e: e.affine_select(out=Mtri[:], in_=negbig[:], pattern=[[-1, 128]], compare_op=ALU.is_ge,
                                          fill=0.0, base=-1, channel_multiplier=1), reads=[Rc], writes=[Rc])
    op("pool", lambda e: e.tensor_copy(out=Ucsb[:], in_=Ucs[:]), reads=[Rc], writes=[Rc])
    op("pool", lambda e: e.tensor_copy(out=Ustb[:], in_=Ust[:]), reads=[Rc], writes=[Rc])
    for h in range(4):
        op("dve", lambda e, h=h: e.tensor_scalar(out=bts[h][:], in0=iot[:], scalar1=SLOPES[h], scalar2=None,
                                                op0=ALU.mult), reads=[Rc], writes=[Rc])
    d0 = cx.dsem(0)
    dma("sp", nw_b[:], norm_w.partition_broadcast(128), d0, writes=[Rl])
    dma("sp", fnw_b[:], final_norm_w.partition_broadcast(128), d0, writes=[Rl])
    for i, a in enumerate((lq1, lk1, lq2, lk2)):
        dma("sp", lam4[:, i, :], a.partition_broadcast(128), d0, writes=[Rl])
    dma("sp", wcol_d[:], diff_subln_w.rearrange("o e -> e o"), d0, writes=[Rl])
    dma("sp", wcol_g[:], gla_norm_w.rearrange("o e -> e o"), d0, writes=[Rl])
    op("dve", lambda e: e.tensor_tensor(out=lamp[:, 0, :], in0=lam4[:, 0, :], in1=lam4[:, 1, :], op=ALU.mult),
       reads=[Rl], writes=[Rl])
    op("dve", lambda e: e.tensor_tensor(out=lamp[:, 1, :], in0=lam4[:, 2, :], in1=lam4[:, 3, :], op=ALU.mult),
       reads=[Rl], writes=[Rl])
    op("act", lambda e: e.activation(out=lam4[:, 0, :], in_=lamp[:, 0, :], func=AF.Identity, accum_out=lams[:, 0:1]),
       reads=[Rl], writes=[Rl])
    op("act", lambda e: e.activation(out=lam4[:, 1, :], in_=lamp[:, 1, :], func=AF.Identity, accum_out=lams[:, 1:2]),
       reads=[Rl], writes=[Rl])
    op("act", lambda e: e.activation(out=lams[:, 2:4], in_=lams[:, 0:2], func=AF.Exp), reads=[Rl], writes=[Rl])
    op("dve", lambda e: e.tensor_tensor(out=lams[:, 4:5], in0=lams[:, 2:3], in1=lams[:, 3:4], op=ALU.subtract),
       reads=[Rl], writes=[Rl])
    op("dve", lambda e: e.tensor_scalar(out=neglam[:], in0=lams[:, 4:5], scalar1=LAM_INIT, scalar2=-1.0,
                                        op0=ALU.add, op1=ALU.mult), reads=[Rl], writes=[Rl])
    op("dve", lambda e: e.tensor_scalar(out=wcol_d[:], in0=wcol_d[:], scalar1=1.0 - LAM_INIT, scalar2=None,
                                        op0=ALU.mult), reads=[Rl], writes=[Rl])
    cx.barrier()

    with ExitStack() as st:
        w_bf = sb(st, "w_bf", [128, 8, INC], BF16)
        wst = sb(st, "wst", [128, 2, INC], F32)
        xs = sb(st, "xs", [128, 4, D], F32)
        junk = sb(st, "junk", [128, D], F32)
        hb = sb(st, "hb", [128, 4, D], BF16)
        hT = sb(st, "hT", [128, 2, 8, 512], BF16)
        fstg = sb(st, "fstg", [128, 8, 512], BF16)
        tstg = sb(st, "tstg", [128, 4, 512], BF16)
        gastg = sb(st, "gastg", [16, 2, 512], BF16)
        stat = sb(st, "stat", [128, 4, 4], F32)
        tp = [pst(st, "tp%d" % i, [128, 8, 128], BF16) for i in range(2)]
        pf = [pst(st, "pf%d" % i, [128, 512], F32) for i in range(6)]

        Rw = [R("w%d" % k) for k in range(8)]
        Rwst = [R("wst%d" % k) for k in range(2)]
        Rx = [R("x%d" % i) for i in range(4)]
        for g in range(4):
            dma("sp", xs[:, g, :], x[g * 128:(g + 1) * 128, :], cx.dsem(3 + g), writes=[Rx[g]])
        for kc in range(8):
            s = kc % 2
            dma("sp", wst[:, s, :], w_in[kc * 128:(kc + 1) * 128, :], cx.dsem(1 + s), writes=[Rwst[s]])
            op("dve" if kc % 2 == 0 else "pool",
               lambda e, kc=kc, s=s: e.tensor_copy(out=w_bf[:, kc, :], in_=wst[:, s, :]),
               reads=[Rwst[s]], writes=[Rw[kc]])

        Rst = [R("stat%d" % i) for i in range(4)]
        Rjunk = R("junk")
        Rhb = [R("hb%d" % i) for i in range(4)]
        Rtp = [R("tp%d" % i) for i in range(2)]
        RhT = [[R("hT%d_%d" % (s, t)) for t in range(4)] for s in range(2)]
        Rpf = [R("pf%d" % i) for i in range(6)]
        Rf = [R("fstg%d" % i) for i in range(8)]
        Rt = [R("tstg%d" % i) for i in range(4)]
        Rga = [R("gastg%d" % i) for i in range(2)]
        cnt = dict(pf=0, f=0, t=0, ga=0)

        feat = []
        for h in range(4):
            feat.append((C_DQ + 128 * h, "scale", s_qT[h]))
        for h in range(4):
            feat.append((C_DK + 128 * h, "copy", s_kT[h]))
        for h in range(4):
            feat.append((C_DG + 128 * h, "silu", s_gT[h]))
        for p in range(2):
            feat.append((C_GQ + 128 * p, "scale", s_gqT[p]))
        for p in range(2):
            feat.append((C_GK + 128 * p, "copy", s_gkT[p]))
        for h in range(4):
            feat.append((C_GG + 128 * h, "silu", s_ggT[h]))
        feat.append((C_GA, "ga", s_gaT))
        tokm = [(C_DV, 512, s_v), (C_GV, 512, s_gv), (C_GK, 256, s_gk)]

        def xload1(j):
            for tt in range(4):
                g = 4 * j + tt
                xsl = g % 4
                dma("sp", xs[:, xsl, :], x[g * 128:(g + 1) * 128, :], cx.dsem(3 + xsl), writes=[Rx[xsl]])

        def normA(j, tts=(0, 1, 2, 3)):
            for tt in tts:
                g = 4 * j + tt
                xsl = g % 4
                op("act", lambda e: e.activation(out=junk[:], in_=xs[:, xsl, :], func=AF.Square,
                                                 accum_out=stat[:, xsl, 0:1]),
                   reads=[Rx[xsl]], writes=[Rjunk, Rst[xsl]])
                op("act", lambda e: e.activation(out=stat[:, xsl, 1:2], in_=stat[:, xsl, 0:1], func=AF.Ln,
                                                 scale=1.0 / D, bias=epsc[:]),
                   reads=[Rst[xsl]], writes=[Rst[xsl]])
                op("act", lambda e: e.activation(out=stat[:, xsl, 2:3], in_=stat[:, xsl, 1:2], func=AF.Exp,
                                                 scale=-0.5),
                   reads=[Rst[xsl]], writes=[Rst[xsl]])
                op("dve", lambda e: e.scalar_tensor_tensor(out=hb[:, xsl, :], in0=xs[:, xsl, :],
                                                           scalar=stat[:, xsl, 2:3], in1=nw_b[:],
                                                           op0=ALU.mult, op1=ALU.mult),
                   reads=[Rx[xsl], Rst[xsl]], writes=[Rhb[xsl]])

        def normB(j):
            sl = j % 2
            for tt in range(4):
                g = 4 * j + tt
                xsl = g % 4
                hs = g % 2
                for kc in range(8):
                    op("pe", lambda e: e.transpose(out=tp[hs][:, kc, :], in_=hb[:, xsl, kc * 128:(kc + 1) * 128],
                                                   identity=ident[:]),
                       reads=[Rhb[xsl]], writes=[Rtp[hs]], inc=(kc == 7))
                op("dve" if tt % 2 == 0 else "act",
                   (lambda e: e.tensor_copy(out=hT[:, sl, :, tt * 128:(tt + 1) * 128], in_=tp[hs][:, :, :]))
                   if tt % 2 == 0 else
                   (lambda e: e.activation(out=hT[:, sl, :, tt * 128:(tt + 1) * 128], in_=tp[hs][:, :, :],
                                           func=AF.Copy)),
                   reads=[Rtp[hs]], writes=[RhT[sl][tt]])

        normA(0)
        normB(0)
        for j in range(NB):
            sl = j % 2
            if j + 1 < NB:
                xload1(j + 1)
            for fi, (c0, kind, dst) in enumerate(feat):
                if j + 1 < NB and fi % 4 == 2 and fi // 4 < 4:
                    normA(j + 1, (fi // 4,))
                b = cnt["pf"] % 6
                cnt["pf"] += 1
                M = 16 if kind == "ga" else 128
                for kc in range(8):
                    op("pe", lambda e: e.matmul(pf[b][0:M, :], lhsT=w_bf[:, kc, c0:c0 + M], rhs=hT[:, sl, kc, :],
                                                start=(kc == 0), stop=(kc == 7)),
                       reads=[Rw[kc]] + RhT[sl], writes=[Rpf[b]], inc=(kc == 7))
                if kind == "ga":
                    s = cnt["ga"] % 2
                    cnt["ga"] += 1
                    op("dve", lambda e: e.tensor_copy(out=gastg[:, s, :], in_=pf[b][0:16, :]),
                       reads=[Rpf[b]], writes=[Rga[s]])
                    dma("sp", dst[:, j * 512:(j + 1) * 512], gastg[:, s, :], cx.dsem(7 + s), reads=[Rga[s]])
                    continue
                s = cnt["f"] % 8
                cnt["f"] += 1
                if kind == "scale":
                    op("dve", lambda e: e.tensor_scalar(out=fstg[:, s, :], in0=pf[b][:, :], scalar1=0.125,
                                                        scalar2=None, op0=ALU.mult),
                       reads=[Rpf[b]], writes=[Rf[s]])
                elif kind == "copy":
                    op("dve", lambda e: e.tensor_copy(out=fstg[:, s, :], in_=pf[b][:, :]),
                       reads=[Rpf[b]], writes=[Rf[s]])
                else:
                    op("act", lambda e: e.activation(out=fstg[:, s, :], in_=pf[b][:, :], func=AF.Silu),
                       reads=[Rpf[b]], writes=[Rf[s]])
                dma("sp", dst[:, j * 512:(j + 1) * 512], fstg[:, s, :], cx.dsem(9 + s), reads=[Rf[s]])
            if j + 1 < NB:
                normB(j + 1)
            for tt in range(4):
                g = 4 * j + tt
                for (c0, n, dst) in tokm:
                    b = cnt["pf"] % 6
                    cnt["pf"] += 1
                    for kc in range(8):
                        op("pe", lambda e: e.matmul(pf[b][:, 0:n], lhsT=hT[:, sl, kc, tt * 128:(tt + 1) * 128],
                                                    rhs=w_bf[:, kc, c0:c0 + n], start=(kc == 0), stop=(kc == 7)),
                           reads=[Rw[kc], RhT[sl][tt]], writes=[Rpf[b]], inc=(kc == 7))
                    s = cnt["t"] % 4
                    cnt["t"] += 1
                    op("dve" if cnt["t"] % 3 else "act",
                       (lambda e: e.tensor_copy(out=tstg[:, s, 0:n], in_=pf[b][:, 0:n])) if cnt["t"] % 3 else
                       (lambda e: e.activation(out=tstg[:, s, 0:n], in_=pf[b][:, 0:n], func=AF.Copy)),
                       reads=[Rpf[b]], writes=[Rt[s]])
                    dma("sp", dst[g * 128:(g + 1) * 128, :], tstg[:, s, 0:n], cx.dsem(17 + s), reads=[Rt[s]])
        cx.barrier()

    if dbg == 1:
        gst.close()
        cx.stack.close()
        return nc

    with ExitStack() as st:
        KT = sb(st, "KT", [128, 2, T], BF16)
        QT = sb(st, "QT", [128, 2, T], BF16)
        GT = sb(st, "GT", [128, 2, T], BF16)
        V = sb(st, "V", [128, 2, NT, 128], BF16)
        Pt = sb(st, "Pt", [128, 3, 2, 512], BF16)
        osb2 = sb(st, "osb2", [128, 2, 512], F32)
        zsb = sb(st, "zsb", [128, 2, 512], F32)
        ob = sb(st, "ob", [128, 4, 512], F32)
        sqb = sb(st, "sqb", [128, 4, 512], BF16)
        lnb = sb(st, "lnb", [128, 4, 512], F32)
        rsb = sb(st, "rsb", [128, 4, 512], F32)
        tmpb = sb(st, "tmpb", [128, 512], F32)
        ystg = sb(st, "ystg", [128, 4, 512], BF16)
        Sall = pst(st, "Sall", [128, 4, 512], F32)
        accO = pst(st, "accO", [128, 2, 512], F32)
        accZ = pst(st, "accZ", [128, 2, 512], F32)
        acc = [accO[:, 0, :], accO[:, 1, :], accZ[:, 0, :], accZ[:, 1, :]]

        RK = [R("K%d" % i) for i in range(2)]
        RQ = [R("Q%d" % i) for i in range(2)]
        RG = [R("G%d" % i) for i in range(2)]
        RV = [R("V%d" % i) for i in range(2)]
        RS = [R("S%d" % i) for i in range(2)]
        RP = [R("P%d" % i) for i in range(3)]
        RA = [R("acc%d" % i) for i in range(4)]
        Ros2 = [R("osb2_%d" % i) for i in range(2)]
        Rzs = [R("zsb%d" % i) for i in range(2)]
        Rob = [R("ob%d" % i) for i in range(4)]
        Rsq = [R("sq%d" % i) for i in range(4)]
        Rln = [R("ln%d" % i) for i in range(4)]
        Rrs = [R("rs%d" % i) for i in range(4)]
        Rtmp = R("tmp")
        Rys = [R("ys%d" % i) for i in range(4)]

        def load_head(h):
            hs = h % 2
            CH = min(T, 2048)
            for c in range(0, T, CH):
                dma("sp", KT[:, hs, c:c + CH], s_kT[h][:, c:c + CH], cx.dsem(21 + hs), writes=[RK[hs]])
                dma("sp", QT[:, hs, c:c + CH], s_qT[h][:, c:c + CH], cx.dsem(23 + hs), writes=[RQ[hs]])
            for t0 in range(0, NT, 8):
                dma("sp", V[:, hs, t0:t0 + 8, :],
                    s_v[t0 * 128:(t0 + 8) * 128, h * 128:(h + 1) * 128].rearrange("(t p) e -> p t e", p=128),
                    cx.dsem(25 + hs), writes=[RV[hs]])
            for c in range(0, T, CH):
                dma("sp", GT[:, hs, c:c + CH], s_gT[h][:, c:c + CH], cx.dsem(27 + hs), writes=[RG[hs]])

        pairs = []
        qbc = 0
        for h in range(4):
            Nq = 256 if h == 0 else 512
            r = Nq // 128
            for qb in range(T // Nq):
                nk = (qb + 1) * r
                kt_lo = 0
                if WIN_THR is not None:
                    while kt_lo < nk - r and SLOPES[h] * (qb * Nq - (kt_lo * 128 + 127)) >= WIN_THR:
                        kt_lo += 1
                for kt in range(kt_lo, nk):
                    pairs.append((h, qb, kt, nk, Nq, r, qbc, kt_lo))
                qbc += 1

        pending = []
        sc = [0]

        def defer(at, fn):
            pending.append((at, len(pending), fn))
            pending.sort(key=lambda t: (t[0], t[1]))

        def epilogue(n, h, qb, Nq, qbc):
            hs = h % 2
            q0 = qb * Nq
            osl = qbc % 4
            op("dve", lambda e: e.tensor_copy(out=osb2[:, :, :Nq], in_=accO[:, :, :Nq]), reads=[RA[0], RA[1]],
               writes=[Ros2[0], Ros2[1]])
            op("act", lambda e: e.activation(out=zsb[:, :, :Nq], in_=accZ[:, :, :Nq], func=AF.Copy),
               reads=[RA[2], RA[3]], writes=[Rzs[0], Rzs[1]])
            for i in range(2):
                op("dve", lambda e: e.reciprocal(out=zsb[:, i, :Nq], in_=zsb[:, i, :Nq]), reads=[Rzs[i]], writes=[Rzs[i]])
                op("dve", lambda e: e.tensor_tensor(out=osb2[:, i, :Nq], in0=osb2[:, i, :Nq], in1=zsb[:, i, :Nq],
                                                    op=ALU.mult), reads=[Ros2[i], Rzs[i]], writes=[Ros2[i]])
            op("dve", lambda e: e.scalar_tensor_tensor(out=ob[:, osl, :Nq], in0=osb2[:, 1, :Nq], scalar=neglam[:],
                                                       in1=osb2[:, 0, :Nq], op0=ALU.mult, op1=ALU.add),
               reads=[Ros2[0], Ros2[1], Rl], writes=[Rob[osl]])
            op("pool", lambda e: e.tensor_tensor(out=sqb[:, osl, :Nq], in0=ob[:, osl, :Nq], in1=ob[:, osl, :Nq],
                                                 op=ALU.mult), reads=[Rob[osl]], writes=[Rsq[osl]])
            box = {}

            def f_pe():
                s = sc[0] % 2
                sc[0] += 1
                box["s"] = s
                op("pe", lambda e: e.matmul(Sall[:, 2 * s, :Nq], lhsT=onesmb[:], rhs=sqb[:, osl, :Nq], start=True,
                                            stop=True), reads=[Rsq[osl], Rc], writes=[RS[s]])

            def f_act():
                s = box["s"]
                op("act", lambda e: e.activation(out=lnb[:, osl, :Nq], in_=Sall[:, 2 * s, :Nq], func=AF.Ln,
                                                 bias=epsc[:], scale=1.0), reads=[RS[s]], writes=[Rln[osl]])
                op("act", lambda e: e.activation(out=rsb[:, osl, :Nq], in_=lnb[:, osl, :Nq], func=AF.Exp, scale=-0.5),
                   reads=[Rln[osl]], writes=[Rrs[osl]])

            def f_dve():
                op("dve", lambda e: e.tensor_tensor(out=tmpb[:, :Nq], in0=ob[:, osl, :Nq], in1=rsb[:, osl, :Nq],
                                                    op=ALU.mult), reads=[Rob[osl], Rrs[osl]], writes=[Rtmp])
                op("dve", lambda e: e.scalar_tensor_tensor(out=ystg[:, osl, :Nq], in0=tmpb[:, :Nq], scalar=wcol_d[:],
                                                           in1=GT[:, hs, q0:q0 + Nq], op0=ALU.mult, op1=ALU.mult),
                   reads=[Rtmp, RG[hs], Rl], writes=[Rys[osl]])
                dma("sp", s_yT[h][:, q0:q0 + Nq], ystg[:, osl, :Nq], cx.dsem(52 + osl), reads=[Rys[osl]])

            defer(n + 13, f_pe)
            defer(n + 13, f_act)
            defer(n + 15, f_dve)

        LA = 1
        NPR = len(pairs)
        load_head(0)
        for n in range(NPR + LA):
            while pending and pending[0][0] <= n:
                pending.pop(0)[2]()
            if n < NPR:
                h, qb, kt, nk, Nq, r, qbc, kt_lo = pairs[n]
                hs = h % 2
                if qb == 0 and kt == kt_lo and h + 1 < 4:
                    cnt_h = sum(1 for p in pairs if p[0] == h)

                    def ld(h=h):
                        while pending:
                            pending.pop(0)[2]()
                        load_head(h + 1)
                    defer(n + min(20, max(1, cnt_h - 2)), ld)
                q0 = qb * Nq
                s = sc[0] % 2
                sc[0] += 1
                sP = n % 3
                jd = kt - r * qb
                cl = 128 * jd if jd > 0 else 0
                diag = jd >= 0
                for i in range(2):
                    op("pe", lambda e: e.matmul(Sall[:, 2 * s + i, cl:Nq],
                                                lhsT=KT[64 * i:64 * i + 64, hs, kt * 128:(kt + 1) * 128],
                                                rhs=QT[64 * i:64 * i + 64, hs, q0 + cl:q0 + Nq], start=True,
                                                stop=not diag),
                       reads=[RK[hs], RQ[hs]], writes=[RS[s]], inc=(i == 1 and not diag))
                if diag:
                    for i in range(2):
                        op("pe", lambda e: e.matmul(Sall[:, 2 * s + i, cl:cl + 128], lhsT=ident[:], rhs=Mtri[:],
                                                    start=False, stop=True),
                           reads=[Rc], writes=[RS[s]], inc=(i == 1))
                jj = r * qb - kt + 3
                op("act", lambda e: e.activation(out=Pt[:, sP, :, cl:Nq], in_=Sall[:, 2 * s:2 * s + 2, cl:Nq],
                                                 func=AF.Exp, bias=bts[h][:, jj:jj + 1], scale=1.0),
                   reads=[RS[s], Rc], writes=[RP[sP]])
            m = n - LA
            if m >= 0:
                h, qb, kt, nk, Nq, r, qbc, kt_lo = pairs[m]
                hs = h % 2
                sP = m % 3
                jd = kt - r * qb
                cl = 128 * jd if jd > 0 else 0
                for i in range(2):
                    op("pe", lambda e: e.matmul(acc[i][:, cl:Nq], lhsT=V[:, hs, kt, :], rhs=Pt[:, sP, i, cl:Nq],
                                                start=(kt == kt_lo), stop=(kt == nk - 1)),
                       reads=[RV[hs], RP[sP]], writes=[RA[i]], inc=False)
                for i in range(2):
                    op("pe", lambda e: e.matmul(acc[2 + i][:, cl:Nq], lhsT=onesb[:], rhs=Pt[:, sP, i, cl:Nq],
                                                start=(kt == kt_lo), stop=(kt == nk - 1)),
                       reads=[RP[sP], Rc], writes=[RA[2 + i]], inc=(i == 1))
                if kt == nk - 1:
                    epilogue(n, h, qb, Nq, qbc)
        while pending:
            pending.pop(0)[2]()
        cx.barrier()

    if dbg == 2:
        gst.close()
        cx.stack.close()
        return nc

    if not FUSE4:
        wo_bf = sb(gst, "wo_bf", [128, 8, D], BF16)
    with ExitStack() as st:
        gqTb = sb(st, "gqTb", [128, 3, 2, 512], BF16)
        gkTb = sb(st, "gkTb", [128, 3, 2, 512], BF16)
        ggTb = sb(st, "ggTb", [128, 3, 4, 512], BF16)
        gvb = sb(st, "gvb", [128, 3, 4, 512], BF16)
        gkb = sb(st, "gkb", [128, 3, 4, 256], BF16)
        gab = sb(st, "gab", [17, 3, 512], BF16)
        wup = sb(st, "wup", [17, 256], BF16)
        wupf = sb(st, "wupf", [17, 256], F32)
        gmask4 = sb(st, "gmask4", [128, 4, 128], F32)
        e1b = sb(st, "e1b", [128, 256], F32)
        ltb = sb(st, "ltb", [128, 2, 256], F32)
        lhi = sb(st, "lhi", [128, 2, 256], BF16)
        llo = sb(st, "llo", [128, 2, 256], BF16)
        ebT = sb(st, "ebT", [128, 2, 2, 128], F32)
        enbT = sb(st, "enbT", [128, 2, 128], F32)
        ksd = sb(st, "ksd", [128, 256], F32)
        qz = sb(st, "qz", [128, 2, 4, 128], BF16)
        kin = sb(st, "kin", [128, 2, 2, 128], BF16)
        kst = sb(st, "kst", [128, 2, 256], BF16)
        aTb = sb(st, "aTb", [128, 4, 128], BF16)
        Sf = sb(st, "Sf", [128, 2, 128], F32)
        Sbf = sb(st, "Sbf", [128, 2, 2, 128], BF16)
        osb = sb(st, "osb", [128, 4, 512], F32)
        sqg = sb(st, "sqg", [128, 2, 512], BF16)
        lng = sb(st, "lng", [128, 512], F32)
        rsg = sb(st, "rsg", [128, 2, 512], F32)
        tmpg = sb(st, "tmpg", [128, 512], F32)
        yg = sb(st, "yg", [128, 8, 4, 128], BF16)
        pA_ = pst(st, "pA", [128, 512], F32)
        pB_ = pst(st, "pB", [128, 512], F32)
        pC_ = pst(st, "pC", [128, 512], F32)
        pD_ = pst(st, "pD", [128, 512], F32)
        pE_ = pst(st, "pE", [128, 512], F32)
        pF_ = pst(st, "pF", [128, 512], F32)
        pG = pst(st, "pG", [128, 512], F32)
        pA = pA_[:, 0:256]
        pB = pB_[:, 0:256].rearrange("p (a b) -> p a b", a=2)
        pC = pC_[:, 0:256]
        pD = pD_[:, :].rearrange("p (a b) -> p a b", a=4)
        pE = pE_[:, :].rearrange("p (a b) -> p a b", a=4)
        pF = pF_[:, :].rearrange("p (a b) -> p a b", a=2)

        Rblk = [[R("gblk%d_%d" % (k, s)) for s in range(3)] for k in range(6)]
        Rg0 = R("g0")
        RA, RB, RC, RD, RE, RF, RG_ = [R("pg%d" % i) for i in range(7)]
        Re1 = R("e1")
        Rlt = [R("lt%d" % i) for i in range(2)]
        Rlh = [R("lh%d" % i) for i in range(2)]
        Rll = [R("ll%d" % i) for i in range(2)]
        Reb = [R("eb%d" % i) for i in range(2)]
        Renb = R("enb")
        Rksd = R("ksd")
        Rqin = [R("qin%d" % i) for i in range(2)]
        Rkin = [R("kin%d" % i) for i in range(2)]
        Rkst = [R("kst%d" % i) for i in range(2)]
        RaT = R("aT")
        RSf = R("Sf")
        RSf4 = [R("Sf%d" % i) for i in range(4)]
        RSb = [R("Sb%d" % i) for i in range(2)]
        Ros = [R("os%d" % i) for i in range(4)]
        Rsqg = [R("sqg%d" % i) for i in range(2)]
        Rlng = R("lng")
        Rrsg = [R("rsg%d" % i) for i in range(2)]
        Rtmpg = R("tmpg")
        Ryg = [R("yg%d" % i) for i in range(8)]

        op("pool", lambda e: e.memset(gab[:], 1.0), writes=[Rblk[5][0], Rblk[5][1], Rblk[5][2]])
        op("pool", lambda e: e.memset(Sf[:], 0.0), writes=[RSf])
        op("pool", lambda e: e.memset(Sbf[:], 0.0), writes=[RSb[0], RSb[1]])
        op("pool", lambda e: e.memset(qz[:], 0.0), writes=[Rqin[0], Rqin[1]])
        for hh in range(4):
            op("pool", lambda e: e.tensor_copy(out=gmask4[:, hh, :], in_=gmask[:]), reads=[Rc], writes=[Rg0])
        Rwupf = R("wupf")
        dma("sp", wupf[0:16, :], w_gate_up, cx.dsem(0), writes=[Rwupf])
        dma("sp", wupf[16:17, :], b_gate, cx.dsem(0), writes=[Rwupf])
        op("pool", lambda e: e.tensor_copy(out=wup[:], in_=wupf[:]), reads=[Rwupf], writes=[Rg0])

        def gload(j):
            s = j % 3
            c = slice(j * 512, (j + 1) * 512)
            dma("sp", gab[0:16, s, :], s_gaT[:, c], cx.dsem(31 + s), writes=[Rblk[5][s]])
            dma("sp", gqTb[:, s, :, :], s_gqT[:, :, c].rearrange("c p t -> p c t"), cx.dsem(34 + s), writes=[Rblk[0][s]])
            dma("sp", gkTb[:, s, :, :], s_gkT[:, :, c].rearrange("c p t -> p c t"), cx.dsem(37 + s), writes=[Rblk[1][s]])
            dma("sp", gkb[:, s, :, :], s_gk[c, :].rearrange("(t p) e -> p t e", p=128), cx.dsem(40 + s),
                writes=[Rblk[4][s]])
            dma("sp", gvb[:, s, :, :], s_gv[c, :].rearrange("(t p) e -> p t e", p=128), cx.dsem(43 + s),
                writes=[Rblk[3][s]])
            dma("sp", ggTb[:, s, :, :], s_ggT[:, :, c].rearrange("c p t -> p c t"), cx.dsem(46 + s), writes=[Rblk[2][s]])

        def s1a(t):
            j, tt = divmod(t, 4)
            s = j % 3
            c0 = tt * 128
            ls = t % 2
            op("pe", lambda e: e.matmul(pA[:, :], lhsT=gab[0:17, s, c0:c0 + 128], rhs=wup[0:17, :], start=True, stop=True),
               reads=[Rblk[5][s], Rg0], writes=[RA])
            op("act", lambda e: e.activation(out=e1b[:], in_=pA[:, :], func=AF.Exp, scale=-1.0), reads=[RA], writes=[Re1])
            op("act", lambda e: e.activation(out=ltb[:, ls, :], in_=e1b[:], func=AF.Ln, bias=onesf[:, 0:1], scale=1.0),
               reads=[Re1, Rc], writes=[Rlt[ls]])
            op("dve", lambda e: e.tensor_copy(out=lhi[:, ls, :], in_=ltb[:, ls, :]), reads=[Rlt[ls]], writes=[Rlh[ls]])
            op("pool", lambda e: e.tensor_tensor(out=llo[:, ls, :], in0=ltb[:, ls, :], in1=lhi[:, ls, :],
                                                 op=ALU.subtract), reads=[Rlt[ls], Rlh[ls]], writes=[Rll[ls]])

        def s1b(t):
            j, tt = divmod(t, 4)
            s = j % 3
            c0 = tt * 128
            ls = t % 2
            for p in range(2):
                op("pe", lambda e: e.matmul(pB[:, p, :], lhsT=lhi[:, ls, p * 128:(p + 1) * 128], rhs=Ucsb[:],
                                            start=True, stop=False), reads=[Rlh[ls], Rc], writes=[RB], inc=False)
                op("pe", lambda e: e.matmul(pB[:, p, :], lhsT=llo[:, ls, p * 128:(p + 1) * 128], rhs=Ucsb[:],
                                            start=False, stop=True), reads=[Rll[ls], Rc], writes=[RB])
            op("pe", lambda e: e.matmul(pC[:, :], lhsT=Ustb[:], rhs=lhi[:, ls, :], start=True, stop=False),
               reads=[Rlh[ls], Rc], writes=[RC], inc=False)
            op("pe", lambda e: e.matmul(pC[:, :], lhsT=Ustb[:], rhs=llo[:, ls, :], start=False, stop=True),
               reads=[Rll[ls], Rc], writes=[RC])
            op("act", lambda e: e.activation(out=ebT[:, ls, :, :], in_=pB[:, :, :], func=AF.Exp), reads=[RB], writes=[Reb[ls]])
            op("act", lambda e: e.activation(out=enbT[:, :, :], in_=pB[:, :, :], func=AF.Exp, scale=-1.0),
               reads=[RB], writes=[Renb])
            op("act", lambda e: e.activation(out=ksd[:], in_=pC[:, :], func=AF.Exp), reads=[RC], writes=[Rksd])
            for h2 in range(2):
                r0 = 64 * h2
                op("dve", lambda e: e.tensor_tensor(out=qz[r0:r0 + 64, ls, h2::2, :],
                                                    in0=gqTb[r0:r0 + 64, s, :, c0:c0 + 128],
                                                    in1=ebT[r0:r0 + 64, ls, :, :], op=ALU.mult),
                   reads=[Rblk[0][s], Reb[ls]], writes=[Rqin[ls]])
            op("pool", lambda e: e.tensor_tensor(out=kin[:, ls, :, :], in0=gkTb[:, s, :, c0:c0 + 128], in1=enbT[:, :, :],
                                                 op=ALU.mult), reads=[Rblk[1][s], Renb], writes=[Rkin[ls]])
            op("pool", lambda e: e.tensor_tensor(out=kst[:, ls, :], in0=gkb[:, s, tt, :], in1=ksd[:], op=ALU.mult),
               reads=[Rblk[4][s], Rksd], writes=[Rkst[ls]])

        def s2a(t):
            j, tt = divmod(t, 4)
            s = j % 3
            ls = t % 2
            cur, nxt = t % 2, (t + 1) % 2
            for hh in range(4):
                p, r0 = hh // 2, 64 * (hh % 2)
                op("pe", lambda e: e.matmul(pD[:, hh, :], lhsT=kin[:, ls, p, :], rhs=qz[:, ls, hh, :],
                                            start=True, stop=True), reads=[Rkin[ls], Rqin[ls]], writes=[RD])
            for p in range(2):
                op("pe", lambda e: e.matmul(pF[:, p, :], lhsT=kst[:, ls, p * 128:(p + 1) * 128],
                                            rhs=gvb[:, s, tt, p * 256:(p + 1) * 256], start=True, stop=True),
                   reads=[Rkst[ls], Rblk[3][s]], writes=[RF])
            op("dve", lambda e: e.tensor_tensor(out=aTb[:, :, :], in0=pD[:, :, :], in1=gmask4[:, :, :], op=ALU.mult),
               reads=[RD, Rg0], writes=[RaT])
            for hh in range(4):
                p, r0 = hh // 2, 64 * (hh % 2)
                op("pe", lambda e: e.matmul(pE[:, hh, :], lhsT=gvb[:, s, tt, hh * 128:(hh + 1) * 128], rhs=aTb[:, hh, :],
                                            start=True, stop=False), reads=[Rblk[3][s], RaT], writes=[RE])
                op("pe", lambda e: e.matmul(pE[:, hh, :], lhsT=Sbf[:, cur, p, :], rhs=qz[:, ls, hh, :],
                                            start=False, stop=True), reads=[RSb[cur], Rqin[ls]], writes=[RE])
            for hh in range(4):
                p, h2 = hh // 2, hh % 2
                r0 = 64 * h2
                op("dve", lambda e: e.scalar_tensor_tensor(out=Sf[r0:r0 + 64, p, :], in0=Sf[r0:r0 + 64, p, :],
                                                           scalar=ebT[r0:r0 + 64, ls, p, 127:128],
                                                           in1=pF[r0:r0 + 64, p, h2 * 128:(h2 + 1) * 128],
                                                           op0=ALU.mult, op1=ALU.add),
                   reads=[RSf, RSf4[hh], Reb[ls], RF], writes=[RSf4[hh]])
            op("pool", lambda e: e.tensor_copy(out=Sbf[:, nxt, :, :], in_=Sf[:, :, :]), reads=[RSf] + RSf4,
               writes=[RSb[nxt]])
            o4 = t % 4
            op("act", lambda e: e.activation(out=osb[:, o4, :], in_=pE_[:, :], func=AF.Copy), reads=[RE], writes=[Ros[o4]])

        def s2b(t):
            o4, s2 = t % 4, t % 2
            op("pool", lambda e: e.tensor_tensor(out=sqg[:, s2, :], in0=osb[:, o4, :], in1=osb[:, o4, :], op=ALU.mult),
               reads=[Ros[o4]], writes=[Rsqg[s2]])

        def s2c(t):
            s2 = t % 2
            op("pe", lambda e: e.matmul(pG[:, :], lhsT=onesmb[:], rhs=sqg[:, s2, :], start=True, stop=True),
               reads=[Rsqg[s2], Rc], writes=[RG_])
            op("act", lambda e: e.activation(out=lng[:], in_=pG[:, :], func=AF.Ln, bias=epsc[:], scale=1.0),
               reads=[RG_], writes=[Rlng])
            op("act", lambda e: e.activation(out=rsg[:, s2, :], in_=lng[:], func=AF.Exp, scale=-0.5),
               reads=[Rlng], writes=[Rrsg[s2]])

        def s2d(t):
            j, tt = divmod(t, 4)
            s = j % 3
            c0 = tt * 128
            o4, s2 = t % 4, t % 2
            op("dve", lambda e: e.tensor_tensor(out=tmpg[:], in0=osb[:, o4, :], in1=rsg[:, s2, :], op=ALU.mult),
               reads=[Ros[o4], Rrsg[s2]], writes=[Rtmpg])
            s8 = t % 8
            op("dve", lambda e: e.scalar_tensor_tensor(out=yg[:, s8, :, :], in0=tmpg[:], scalar=wcol_g[:],
                                                       in1=ggTb[:, s, :, c0:c0 + 128], op0=ALU.mult, op1=ALU.mult),
               reads=[Rtmpg, Rblk[2][s], Rl], writes=[Ryg[s8]])
            if not FUSE4:
                dma("sp", s_yT[4:8, :, t * 128:(t + 1) * 128].rearrange("h p t -> p h t"), yg[:, s8, :, :],
                    cx.dsem(49 + s2), reads=[Ryg[s8]])

        if not FUSE4:
            wst2g = sb(st, "wst2g", [128, 2, D], F32)
            Rwo = [R("wo%d" % k) for k in range(8)]
            Rwst2 = [R("wst2_%d" % k) for k in range(2)]

            def wchunk(kc):
                s = kc % 2
                dma("sp", wst2g[:, s, :], w_out[kc * 128:(kc + 1) * 128, :], cx.dsem(1 + s), writes=[Rwst2[s]])
                op("pool", lambda e: e.tensor_copy(out=wo_bf[:, kc, :], in_=wst2g[:, s, :]), reads=[Rwst2[s]],
                   writes=[Rwo[kc]])
        if FUSE4:
            wo_bf = sb(st, "wo_bf", [128, 8, D], BF16)
            wst2 = sb(st, "wst2", [128, 2, D], F32)
            yTb = sb(st, "yTb", [128, 2, 4, 512], BF16)
            xs4 = sb(st, "xs4", [128, 4, D], F32)
            zb = sb(st, "zb", [128, 2, D], F32)
            junk4 = sb(st, "junk4", [128, D], F32)
            stat4 = sb(st, "stat4", [128, 2, 4], F32)
            ost = sb(st, "ost", [128, 2, D], F32)
            po1 = pst(st, "po1", [128, 512], F32)
            Rwo = [R("wo%d" % k) for k in range(8)]
            Rwst2 = [R("wst2_%d" % k) for k in range(2)]
            for kc in range(8):
                s = kc % 2
                dma("sp", wst2[:, s, :], w_out[kc * 128:(kc + 1) * 128, :], cx.dsem(1 + s), writes=[Rwst2[s]])
                op("dve" if kc % 2 == 0 else "pool",
                   lambda e: e.tensor_copy(out=wo_bf[:, kc, :], in_=wst2[:, s, :]), reads=[Rwst2[s]], writes=[Rwo[kc]])
            RyT = [R("yT%d" % i) for i in range(2)]
            Rx4 = [R("x4_%d" % i) for i in range(4)]
            Rpo1 = R("po1")
            Rz = [R("z%d" % i) for i in range(2)]
            Rj4 = R("junk4")
            Rs4 = [R("s4_%d" % i) for i in range(2)]
            Ro = [R("ost%d" % i) for i in range(2)]

            def yload(j):
                s = j % 2
                dma("sp", yTb[:, s, :, :], s_yT[0:4, :, j * 512:(j + 1) * 512].rearrange("c p t -> p c t"),
                    cx.dsem(21 + s), writes=[RyT[s]])

            def xload4(g):
                xsl = g % 4
                dma("sp", xs4[:, xsl, :], x[g * 128:(g + 1) * 128, :], cx.dsem(3 + xsl), writes=[Rx4[xsl]])

            def p4tile(g):
                j, tt = divmod(g, 4)
                sl = j % 2
                xsl = g % 4
                z2 = g % 2
                s8 = g % 8
                if tt == 0 and j + 1 < NB:
                    yload(j + 1)
                if g + 2 < NT:
                    xload4(g + 2)
                if P4L < 2:
                    return
                for nh in range(2):
                    for kc in range(8):
                        if kc < 4:
                            lh = yTb[:, sl, kc, tt * 128:(tt + 1) * 128]
                            rd = [RyT[sl], Rwo[kc]]
                        else:
                            lh = yg[:, s8, kc - 4, :]
                            rd = [Ryg[s8], Rwo[kc]]
                        op("pe", lambda e: e.matmul(po1[:, :], lhsT=lh, rhs=wo_bf[:, kc, nh * 512:(nh + 1) * 512],
                                                    start=(kc == 0), stop=(kc == 7)), reads=rd, writes=[Rpo1],
                           inc=(kc == 7))
                    if P4L < 3:
                        continue
                    op("dve", lambda e: e.tensor_tensor(out=zb[:, z2, nh * 512:(nh + 1) * 512], in0=po1[:, :],
                                                        in1=xs4[:, xsl, nh * 512:(nh + 1) * 512], op=ALU.add),
                       reads=[Rpo1, Rx4[xsl]], writes=[Rz[z2]])
                if P4L < 4:
                    return
                op("act", lambda e: e.activation(out=junk4[:], in_=zb[:, z2, :], func=AF.Square,
                                                 accum_out=stat4[:, z2, 0:1]), reads=[Rz[z2]], writes=[Rj4, Rs4[z2]])
                op("act", lambda e: e.activation(out=stat4[:, z2, 1:2], in_=stat4[:, z2, 0:1], func=AF.Ln, scale=1.0 / D,
                                                 bias=epsc[:]), reads=[Rs4[z2]], writes=[Rs4[z2]])
                op("act", lambda e: e.activation(out=stat4[:, z2, 2:3], in_=stat4[:, z2, 1:2], func=AF.Exp, scale=-0.5),
                   reads=[Rs4[z2]], writes=[Rs4[z2]])
                if P4L < 5:
                    return
                op("dve", lambda e: e.scalar_tensor_tensor(out=ost[:, z2, :], in0=zb[:, z2, :], scalar=stat4[:, z2, 2:3],
                                                           in1=fnw_b[:], op0=ALU.mult, op1=ALU.mult),
                   reads=[Rz[z2], Rs4[z2], Rl], writes=[Ro[z2]])
                if P4L < 6:
                    return
                dma("sp", out[g * 128:(g + 1) * 128, :], ost[:, z2, :], cx.dsem(56 + z2), reads=[Ro[z2]])

            yload(0)
            xload4(0)
            xload4(1)

        gload(0)
        for it in range(NT + (7 if FUSE4 else 5)):
            if it % 4 == 1 and (it // 4) + 1 < NB:
                gload(it // 4 + 1)
            if not FUSE4 and 2 <= it < 10:
                wchunk(it - 2)
            if it < NT and GST >= 1:
                s1a(it)
            if 0 <= it - 1 < NT and GST >= 2:
                s1b(it - 1)
            if 0 <= it - 2 < NT and GST >= 3:
                s2a(it - 2)
            if 0 <= it - 3 < NT and GST >= 4:
                s2b(it - 3)
            if 0 <= it - 4 < NT and GST >= 5:
                s2c(it - 4)
            if 0 <= it - 5 < NT and GST >= 6:
                s2d(it - 5)
            if FUSE4 and 0 <= it - 6 < NT and P4L >= 1:
                p4tile(it - 6)
        cx.barrier()

    if dbg == 3:
        gst.close()
        cx.stack.close()
        return nc

    with ExitStack() as st:
        if FUSE4:
            gst.close()
            cx.stack.close()
            return nc
        yTb = sb(st, "yTb", [128, 2, 8, 512], BF16)
        xs4 = sb(st, "xs4", [128, 4, D], F32)
        zb = sb(st, "zb", [128, 2, D], F32)
        junk4 = sb(st, "junk4", [128, D], F32)
        stat4 = sb(st, "stat4", [128, 2, 4], F32)
        ost = sb(st, "ost", [128, 2, D], F32)
        po = [pst(st, "po%d" % i, [128, 512], F32) for i in range(4)]

        RyT = [R("yT%d" % i) for i in range(2)]
        Rx4 = [R("x4_%d" % i) for i in range(4)]
        Rpo = [R("po%d" % i) for i in range(4)]
        Rz = [R("z%d" % i) for i in range(2)]
        Rj4 = R("junk4")
        Rs4 = [R("s4_%d" % i) for i in range(2)]
        Ro = [R("ost%d" % i) for i in range(2)]

        def yload(j):
            s = j % 2
            dma("sp", yTb[:, s, :, :], s_yT[:, :, j * 512:(j + 1) * 512].rearrange("c p t -> p c t"), cx.dsem(21 + s),
                writes=[RyT[s]])

        def xload4(g):
            xsl = g % 4
            dma("sp", xs4[:, xsl, :], x[g * 128:(g + 1) * 128, :], cx.dsem(3 + xsl), writes=[Rx4[xsl]])

        fin4 = [None]
        yload(0)
        xload4(0)
        xload4(1)
        for j in range(NB):
            sl = j % 2
            if j + 1 < NB:
                yload(j + 1)
            for tt in range(4):
                g = 4 * j + tt
                xsl = g % 4
                z2 = g % 2
                if g + 2 < NT:
                    xload4(g + 2)
                for nh in range(2):
                    b = 2 * z2 + nh
                    for kc in range(8):
                        op("pe", lambda e: e.matmul(po[b][:, :], lhsT=yTb[:, sl, kc, tt * 128:(tt + 1) * 128],
                                                    rhs=wo_bf[:, kc, nh * 512:(nh + 1) * 512], start=(kc == 0),
                                                    stop=(kc == 7)), reads=[RyT[sl], Rwo[kc]], writes=[Rpo[b]],
                           inc=(kc == 7))
                    op("dve", lambda e: e.tensor_tensor(out=zb[:, z2, nh * 512:(nh + 1) * 512], in0=po[b][:, :],
                                                        in1=xs4[:, xsl, nh * 512:(nh + 1) * 512], op=ALU.add),
                       reads=[Rpo[b], Rx4[xsl]], writes=[Rz[z2]])
                if fin4[0] is not None:
                    fin4[0]()
                op("act", lambda e: e.activation(out=junk4[:], in_=zb[:, z2, :], func=AF.Square,
                                                 accum_out=stat4[:, z2, 0:1]), reads=[Rz[z2]], writes=[Rj4, Rs4[z2]])
                op("act", lambda e: e.activation(out=stat4[:, z2, 1:2], in_=stat4[:, z2, 0:1], func=AF.Ln, scale=1.0 / D,
                                                 bias=epsc[:]), reads=[Rs4[z2]], writes=[Rs4[z2]])
                op("act", lambda e: e.activation(out=stat4[:, z2, 2:3], in_=stat4[:, z2, 1:2], func=AF.Exp, scale=-0.5),
                   reads=[Rs4[z2]], writes=[Rs4[z2]])

                def fin(g=g, z2=z2):
                    op("dve", lambda e: e.scalar_tensor_tensor(out=ost[:, z2, :], in0=zb[:, z2, :],
                                                               scalar=stat4[:, z2, 2:3], in1=fnw_b[:], op0=ALU.mult,
                                                               op1=ALU.mult),
                       reads=[Rz[z2], Rs4[z2], Rl], writes=[Ro[z2]])
                    dma("sp", out[g * 128:(g + 1) * 128, :], ost[:, z2, :], cx.dsem(7 + z2), reads=[Ro[z2]])
                fin4[0] = fin
        fin4[0]()
        cx.barrier()

    gst.close()
    cx.stack.close()
    return nc


_NC_CACHE = {}


def kernel(**inputs):
    xin = np.asarray(inputs["x"], dtype=np.float32)
    B, T, _ = xin.shape
    if T not in _NC_CACHE:
        _NC_CACHE[T] = build(T)
    nc = _NC_CACHE[T]

    def a2(name, shape):
        return np.ascontiguousarray(np.asarray(inputs[name], dtype=np.float32).reshape(shape))

    shared = dict(
        norm_w=a2("norm_w", (1, D)), w_in=a2("w_in", (D, INC)), w_gate_up=a2("w_gate_up", (16, 256)),
        b_gate=a2("b_gate", (1, 256)), lambda_q1=a2("lambda_q1", (1, 64)), lambda_k1=a2("lambda_k1", (1, 64)),
        lambda_q2=a2("lambda_q2", (1, 64)), lambda_k2=a2("lambda_k2", (1, 64)),
        diff_subln_w=a2("diff_subln_w", (1, 128)), gla_norm_w=a2("gla_norm_w", (1, 128)),
        w_out=a2("w_out", (D, D)), final_norm_w=a2("final_norm_w", (1, D)))
    in_maps = []
    for b in range(B):
        m = dict(shared)
        m["x"] = np.ascontiguousarray(xin[b])
        in_maps.append(m)
    res = run_bass_kernel_spmd(nc, in_maps, core_ids=list(range(B)))
    return np.stack([np.asarray(r["out"], dtype=np.float32) for r in res.results], axis=0)
```

```python
import math
from contextlib import ExitStack

import numpy as np
import concourse.bass as bass
import concourse.mybir as mybir
from concourse.bass_utils import run_bass_kernel_spmd

F32 = mybir.dt.float32
BF16 = mybir.dt.bfloat16
I32 = mybir.dt.int32
AF = mybir.ActivationFunctionType
ALU = mybir.AluOpType

D = 1024
INC = 3600
C_DQ, C_DK, C_DV, C_DG, C_GQ, C_GK, C_GV, C_GG, C_GA = 0, 512, 1024, 1536, 2048, 2304, 2560, 3072, 3584
SLOPES = [2.0 ** (-8.0 * (h + 1) / 4) for h in range(4)]
LAM_INIT = 0.8 - 0.6 * math.exp(-0.3 * 0)
EPS = 1e-6
SEQ = 8192
import os
GST = int(os.environ.get('GST', '6'))
WIN_THR = 96.0
P4L = int(os.environ.get('P4L', '9'))
FUSE4 = False


class R:
    __slots__ = ("name", "w", "r")

    def __init__(self, name):
        self.name = name
        self.w = {}
        self.r = {}


class Ctx:
    def __init__(self, nc):
        self.nc = nc
        self.E = dict(pe=nc.tensor, act=nc.scalar, dve=nc.vector, pool=nc.gpsimd, sp=nc.sync)
        self.sem = {}
        self.cnt = {}
        self.seen = {e: {} for e in self.E}
        self.stack = ExitStack()
        for e in self.E:
            self.newsem(e)
        self.ndsem = 0

    def newsem(self, key):
        s = self.stack.enter_context(self.nc.semaphore("s_" + key))
        self.sem[key] = s
        self.cnt[key] = 0
        return key

    def dsem(self, i):
        key = "d%d" % i
        if key not in self.sem:
            self.newsem(key)
        return key

    def _wait(self, e, deps):
        for k, v in deps.items():
            if v <= 0:
                continue
            if self.seen[e].get(k, 0) < v:
                self.E[e].wait_ge(self.sem[k], v)
                self.seen[e][k] = v

    def _deps(self, e, reads, writes, skip=None):
        deps = {}
        for r in reads:
            for k, v in r.w.items():
                if k == e and e == "pe":
                    continue
                if k == skip:
                    continue
                if deps.get(k, 0) < v:
                    deps[k] = v
        for w in writes:
            dd = w.r if w.r else w.w
            for k, v in dd.items():
                if k == skip or (k == e and e == "pe"):
                    continue
                if deps.get(k, 0) < v:
                    deps[k] = v
        return deps

    def op(self, e, fn, reads=(), writes=(), inc=True):
        self._wait(e, self._deps(e, reads, writes))
        ins = fn(self.E[e])
        if inc:
            self.cnt[e] += 1
            ins.then_inc(self.sem[e], 1)
            c = self.cnt[e]
        else:
            c = self.cnt[e] + 1
        for r in reads:
            r.r[e] = c
        for w in writes:
            w.w = {e: c}
            w.r = {}
        return ins

    def dma(self, q, out, in_, sem, reads=(), writes=()):
        self._wait(q, self._deps(q, reads, writes, skip=sem))
        ins = self.E[q].dma_start(out=out, in_=in_)
        self.cnt[sem] += 16
        ins.then_inc(self.sem[sem], 16)
        c = self.cnt[sem]
        for r in reads:
            r.r[sem] = c
        for w in writes:
            if sem in w.w and not w.r:
                w.w[sem] = c
            else:
                w.w = {sem: c}
            w.r = {}
        return ins

    def barrier(self):
        targets = dict(self.cnt)
        for e in self.E:
            self._wait(e, {k: v for k, v in targets.items() if k != e})


def build(T=SEQ, dbg=False):
    nc = bass.Bass("TRN2", target_bir_lowering=False)
    NB = T // 512
    NT = T // 128
    kind_s = "ExternalOutput" if dbg else "Internal"

    def dram_in(name, shape):
        return nc.dram_tensor(name, shape, F32, kind="ExternalInput").ap()

    x = dram_in("x", [T, D])
    norm_w = dram_in("norm_w", [1, D])
    w_in = dram_in("w_in", [D, INC])
    w_gate_up = dram_in("w_gate_up", [16, 256])
    b_gate = dram_in("b_gate", [1, 256])
    lq1 = dram_in("lambda_q1", [1, 64])
    lk1 = dram_in("lambda_k1", [1, 64])
    lq2 = dram_in("lambda_q2", [1, 64])
    lk2 = dram_in("lambda_k2", [1, 64])
    diff_subln_w = dram_in("diff_subln_w", [1, 128])
    gla_norm_w = dram_in("gla_norm_w", [1, 128])
    w_out = dram_in("w_out", [D, D])
    final_norm_w = dram_in("final_norm_w", [1, D])
    out = nc.dram_tensor("out", [T, D], F32, kind="ExternalOutput").ap()

    def scr(name, shape, dt=BF16):
        return nc.dram_tensor(name, shape, dt, kind=kind_s).ap()

    s_qT = scr("s_qT", [4, 128, T])
    s_kT = scr("s_kT", [4, 128, T])
    s_gT = scr("s_gT", [4, 128, T])
    s_gqT = scr("s_gqT", [2, 128, T])
    s_gkT = scr("s_gkT", [2, 128, T])
    s_ggT = scr("s_ggT", [4, 128, T])
    s_gaT = scr("s_gaT", [16, T])
    s_v = scr("s_v", [T, 512])
    s_gv = scr("s_gv", [T, 512])
    s_gk = scr("s_gk", [T, 256])
    s_yT = scr("s_yT", [8, 128, T])

    cx = Ctx(nc)
    op, dma = cx.op, cx.dma

    gst = ExitStack()

    def sb(st, name, shape, dt):
        return st.enter_context(nc.sbuf_tensor(name, shape, dt))

    def pst(st, name, shape, dt=F32):
        return st.enter_context(nc.psum_tensor(name, shape, dt))

    ident = sb(gst, "ident", [128, 128], BF16)
    onesb = sb(gst, "onesb", [128, 128], BF16)
    onesf = sb(gst, "onesf", [128, 128], F32)
    onesm = sb(gst, "onesm", [128, 128], F32)
    negs = sb(gst, "negs", [128, 128], F32)
    Ucs = sb(gst, "Ucs", [128, 128], F32)
    Ust = sb(gst, "Ust", [128, 128], F32)
    gmask = sb(gst, "gmask", [128, 128], F32)
    onesmb = sb(gst, "onesmb", [128, 128], BF16)
    negbig = sb(gst, "negbig", [128, 128], BF16)
    Mtri = sb(gst, "Mtri", [128, 128], BF16)
    Ucsb = sb(gst, "Ucsb", [128, 128], BF16)
    Ustb = sb(gst, "Ustb", [128, 128], BF16)
    iot = sb(gst, "iot", [128, NT + 4], I32)
    bts = [sb(gst, "bt%d" % h, [128, NT + 4], F32) for h in range(4)]
    nw_b = sb(gst, "nw_b", [128, D], F32)
    fnw_b = sb(gst, "fnw_b", [128, D], F32)
    lam4 = sb(gst, "lam4", [128, 4, 64], F32)
    lamp = sb(gst, "lamp", [128, 2, 64], F32)
    lams = sb(gst, "lams", [128, 8], F32)
    neglam = sb(gst, "neglam", [128, 1], F32)
    wcol_d = sb(gst, "wcol_d", [128, 1], F32)
    wcol_g = sb(gst, "wcol_g", [128, 1], F32)
    epsc = sb(gst, "epsc", [128, 1], F32)
    Rc = R("consts")
    Rl = R("lam")

    P = [("pool", lambda e: e.memset(onesb[:], 1.0)),
         ("pool", lambda e: e.memset(onesf[:], 1.0)),
         ("pool", lambda e: e.memset(onesm[:], 1.0 / 128)),
         ("pool", lambda e: e.memset(negs[:], -1.0 / 16)),
         ("pool", lambda e: e.memset(epsc[:], EPS))]
    for e, f in P:
        op(e, f, writes=[Rc])
    op("pool", lambda e: e.affine_select(out=ident[:], in_=onesb[:], pattern=[[1, 128]], compare_op=ALU.is_equal,
                                          fill=0.0, base=0, channel_multiplier=-1), reads=[Rc], writes=[Rc])
    op("pool", lambda e: e.affine_select(out=Ucs[:], in_=negs[:], pattern=[[1, 128]], compare_op=ALU.is_ge,
                                          fill=0.0, base=0, channel_multiplier=-1), reads=[Rc], writes=[Rc])
    op("pool", lambda e: e.affine_select(out=Ust[:], in_=negs[:], pattern=[[-1, 128]], compare_op=ALU.is_ge,
                                          fill=0.0, base=-1, channel_multiplier=1), reads=[Rc], writes=[Rc])
    op("pool", lambda e: e.affine_select(out=gmask[:], in_=onesf[:], pattern=[[1, 128]], compare_op=ALU.is_ge,
                                          fill=0.0, base=0, channel_multiplier=-1), reads=[Rc], writes=[Rc])
    op("pool", lambda e: e.iota(iot[:], pattern=[[-128, NT + 4]], base=384, channel_multiplier=1),
       reads=[Rc], writes=[Rc])
    op("pool", lambda e: e.tensor_copy(out=onesmb[:], in_=onesm[:]), reads=[Rc], writes=[Rc])
    op("pool", lambda e: e.memset(negbig[:], -30000.0), writes=[Rc])
    op("pool", lambda e: e.affine_select(out=Mtri[:], in_=negbig[:], pattern=[[-1, 128]], compare_op=ALU.is_ge,
                                          fill=0.0, base=-1, channel_multiplier=1), reads=[Rc], writes=[Rc])
    op("pool", lambda e: e.tensor_copy(out=Ucsb[:], in_=Ucs[:]), reads=[Rc], writes=[Rc])
    op("pool", lambda e: e.tensor_copy(out=Ustb[:], in_=Ust[:]), reads=[Rc], writes=[Rc])
    for h in range(4):
        op("dve", lambda e, h=h: e.tensor_scalar(out=bts[h][:], in0=iot[:], scalar1=SLOPES[h], scalar2=None,
                                                op0=ALU.mult), reads=[Rc], writes=[Rc])
    d0 = cx.dsem(0)
    dma("sp", nw_b[:], norm_w.partition_broadcast(128), d0, writes=[Rl])
    dma("sp", fnw_b[:], final_norm_w.partition_broadcast(128), d0, writes=[Rl])
    for i, a in enumerate((lq1, lk1, lq2, lk2)):
        dma("sp", lam4[:, i, :], a.partition_broadcast(128), d0, writes=[Rl])
    dma("sp", wcol_d[:], diff_subln_w.rearrange("o e -> e o"), d0, writes=[Rl])
    dma("sp", wcol_g[:], gla_norm_w.rearrange("o e -> e o"), d0, writes=[Rl])
    op("dve", lambda e: e.tensor_tensor(out=lamp[:, 0, :], in0=lam4[:, 0, :], in1=lam4[:, 1, :], op=ALU.mult),
       reads=[Rl], writes=[Rl])
    op("dve", lambda e: e.tensor_tensor(out=lamp[:, 1, :], in0=lam4[:, 2, :], in1=lam4[:, 3, :], op=ALU.mult),
       reads=[Rl], writes=[Rl])
    op("act", lambda e: e.activation(out=lam4[:, 0, :], in_=lamp[:, 0, :], func=AF.Identity, accum_out=lams[:, 0:1]),
       reads=[Rl], writes=[Rl])
    op("act", lambda e: e.activation(out=lam4[:, 1, :], in_=lamp[:, 1, :], func=AF.Identity, accum_out=lams[:, 1:2]),
       reads=[Rl], writes=[Rl])
    op("act", lambda e: e.activation(out=lams[:, 2:4], in_=lams[:, 0:2], func=AF.Exp), reads=[Rl], writes=[Rl])
    op("dve", lambda e: e.tensor_tensor(out=lams[:, 4:5], in0=lams[:, 2:3], in1=lams[:, 3:4], op=ALU.subtract),
       reads=[Rl], writes=[Rl])
    op("dve", lambda e: e.tensor_scalar(out=neglam[:], in0=lams[:, 4:5], scalar1=LAM_INIT, scalar2=-1.0,
                                        op0=ALU.add, op1=ALU.mult), reads=[Rl], writes=[Rl])
    op("dve", lambda e: e.tensor_scalar(out=wcol_d[:], in0=wcol_d[:], scalar1=1.0 - LAM_INIT, scalar2=None,
                                        op0=ALU.mult), reads=[Rl], writes=[Rl])
    cx.barrier()

    with ExitStack() as st:
        w_bf = sb(st, "w_bf", [128, 8, INC], BF16)
        wst = sb(st, "wst", [128, 2, INC], F32)
        xs = sb(st, "xs", [128, 4, D], F32)
        junk = sb(st, "junk", [128, D], F32)
        hb = sb(st, "hb", [128, 4, D], BF16)
        hT = sb(st, "hT", [128, 2, 8, 512], BF16)
        fstg = sb(st, "fstg", [128, 8, 512], BF16)
        tstg = sb(st, "tstg", [128, 4, 512], BF16)
        gastg = sb(st, "gastg", [16, 2, 512], BF16)
        stat = sb(st, "stat", [128, 4, 4], F32)
        tp = [pst(st, "tp%d" % i, [128, 8, 128], BF16) for i in range(2)]
        pf = [pst(st, "pf%d" % i, [128, 512], F32) for i in range(6)]

        Rw = [R("w%d" % k) for k in range(8)]
        Rwst = [R("wst%d" % k) for k in range(2)]
        Rx = [R("x%d" % i) for i in range(4)]
        for g in range(4):
            dma("sp", xs[:, g, :], x[g * 128:(g + 1) * 128, :], cx.dsem(3 + g), writes=[Rx[g]])
        for kc in range(8):
            s = kc % 2
            dma("sp", wst[:, s, :], w_in[kc * 128:(kc + 1) * 128, :], cx.dsem(1 + s), writes=[Rwst[s]])
            op("dve" if kc % 2 == 0 else "pool",
               lambda e, kc=kc, s=s: e.tensor_copy(out=w_bf[:, kc, :], in_=wst[:, s, :]),
               reads=[Rwst[s]], writes=[Rw[kc]])

        Rst = [R("stat%d" % i) for i in range(4)]
        Rjunk = R("junk")
        Rhb = [R("hb%d" % i) for i in range(4)]
        Rtp = [R("tp%d" % i) for i in range(2)]
        RhT = [[R("hT%d_%d" % (s, t)) for t in range(4)] for s in range(2)]
        Rpf = [R("pf%d" % i) for i in range(6)]
        Rf = [R("fstg%d" % i) for i in range(8)]
        Rt = [R("tstg%d" % i) for i in range(4)]
        Rga = [R("gastg%d" % i) for i in range(2)]
        cnt = dict(pf=0, f=0, t=0, ga=0)

        feat = []
        for h in range(4):
            feat.append((C_DQ + 128 * h, "scale", s_qT[h]))
        for h in range(4):
            feat.append((C_DK + 128 * h, "copy", s_kT[h]))
        for h in range(4):
            feat.append((C_DG + 128 * h, "silu", s_gT[h]))
        for p in range(2):
            feat.append((C_GQ + 128 * p, "scale", s_gqT[p]))
        for p in range(2):
            feat.append((C_GK + 128 * p, "copy", s_gkT[p]))
        for h in range(4):
            feat.append((C_GG + 128 * h, "silu", s_ggT[h]))
        feat.append((C_GA, "ga", s_gaT))
        tokm = [(C_DV, 512, s_v), (C_GV, 512, s_gv), (C_GK, 256, s_gk)]

        def xload1(j):
            for tt in range(4):
                g = 4 * j + tt
                xsl = g % 4
                dma("sp", xs[:, xsl, :], x[g * 128:(g + 1) * 128, :], cx.dsem(3 + xsl), writes=[Rx[xsl]])

        def normA(j, tts=(0, 1, 2, 3)):
            for tt in tts:
                g = 4 * j + tt
                xsl = g % 4
                op("act", lambda e: e.activation(out=junk[:], in_=xs[:, xsl, :], func=AF.Square,
                                                 accum_out=stat[:, xsl, 0:1]),
                   reads=[Rx[xsl]], writes=[Rjunk, Rst[xsl]])
                op("act", lambda e: e.activation(out=stat[:, xsl, 1:2], in_=stat[:, xsl, 0:1], func=AF.Ln,
                                                 scale=1.0 / D, bias=epsc[:]),
                   reads=[Rst[xsl]], writes=[Rst[xsl]])
                op("act", lambda e: e.activation(out=stat[:, xsl, 2:3], in_=stat[:, xsl, 1:2], func=AF.Exp,
                                                 scale=-0.5),
                   reads=[Rst[xsl]], writes=[Rst[xsl]])
                op("dve", lambda e: e.scalar_tensor_tensor(out=hb[:, xsl, :], in0=xs[:, xsl, :],
                                                           scalar=stat[:, xsl, 2:3], in1=nw_b[:],
                                                           op0=ALU.mult, op1=ALU.mult),
                   reads=[Rx[xsl], Rst[xsl]], writes=[Rhb[xsl]])

        def normB(j):
            sl = j % 2
            for tt in range(4):
                g = 4 * j + tt
                xsl = g % 4
                hs = g % 2
                for kc in range(8):
                    op("pe", lambda e: e.transpose(out=tp[hs][:, kc, :], in_=hb[:, xsl, kc * 128:(kc + 1) * 128],
                                                   identity=ident[:]),
                       reads=[Rhb[xsl]], writes=[Rtp[hs]], inc=(kc == 7))
                op("dve" if tt % 2 == 0 else "act",
                   (lambda e: e.tensor_copy(out=hT[:, sl, :, tt * 128:(tt + 1) * 128], in_=tp[hs][:, :, :]))
                   if tt % 2 == 0 else
                   (lambda e: e.activation(out=hT[:, sl, :, tt * 128:(tt + 1) * 128], in_=tp[hs][:, :, :],
                                           func=AF.Copy)),
                   reads=[Rtp[hs]], writes=[RhT[sl][tt]])

        normA(0)
        normB(0)
        for j in range(NB):
            sl = j % 2
            if j + 1 < NB:
                xload1(j + 1)
            for fi, (c0, kind, dst) in enumerate(feat):
                if j + 1 < NB and fi % 4 == 2 and fi // 4 < 4:
                    normA(j + 1, (fi // 4,))
                b = cnt["pf"] % 6
                cnt["pf"] += 1
                M = 16 if kind == "ga" else 128
                for kc in range(8):
                    op("pe", lambda e: e.matmul(pf[b][0:M, :], lhsT=w_bf[:, kc, c0:c0 + M], rhs=hT[:, sl, kc, :],
                                                start=(kc == 0), stop=(kc == 7)),
                       reads=[Rw[kc]] + RhT[sl], writes=[Rpf[b]], inc=(kc == 7))
                if kind == "ga":
                    s = cnt["ga"] % 2
                    cnt["ga"] += 1
                    op("dve", lambda e: e.tensor_copy(out=gastg[:, s, :], in_=pf[b][0:16, :]),
                       reads=[Rpf[b]], writes=[Rga[s]])
                    dma("sp", dst[:, j * 512:(j + 1) * 512], gastg[:, s, :], cx.dsem(7 + s), reads=[Rga[s]])
                    continue
                s = cnt["f"] % 8
                cnt["f"] += 1
                if kind == "scale":
                    op("dve", lambda e: e.tensor_scalar(out=fstg[:, s, :], in0=pf[b][:, :], scalar1=0.125,
                                                        scalar2=None, op0=ALU.mult),
                       reads=[Rpf[b]], writes=[Rf[s]])
                elif kind == "copy":
                    op("dve", lambda e: e.tensor_copy(out=fstg[:, s, :], in_=pf[b][:, :]),
                       reads=[Rpf[b]], writes=[Rf[s]])
                else:
                    op("act", lambda e: e.activation(out=fstg[:, s, :], in_=pf[b][:, :], func=AF.Silu),
                       reads=[Rpf[b]], writes=[Rf[s]])
                dma("sp", dst[:, j * 512:(j + 1) * 512], fstg[:, s, :], cx.dsem(9 + s), reads=[Rf[s]])
            if j + 1 < NB:
                normB(j + 1)
            for tt in range(4):
                g = 4 * j + tt
                for (c0, n, dst) in tokm:
                    b = cnt["pf"] % 6
                    cnt["pf"] += 1
                    for kc in range(8):
                        op("pe", lambda e: e.matmul(pf[b][:, 0:n], lhsT=hT[:, sl, kc, tt * 128:(tt + 1) * 128],
                                                    rhs=w_bf[:, kc, c0:c0 + n], start=(kc == 0), stop=(kc == 7)),
                           reads=[Rw[kc], RhT[sl][tt]], writes=[Rpf[b]], inc=(kc == 7))
                    s = cnt["t"] % 4
                    cnt["t"] += 1
                    op("dve" if cnt["t"] % 3 else "act",
                       (lambda e: e.tensor_copy(out=tstg[:, s, 0:n], in_=pf[b][:, 0:n])) if cnt["t"] % 3 else
                       (lambda e: e.activation(out=tstg[:, s, 0:n], in_=pf[b][:, 0:n], func=AF.Copy)),
                       reads=[Rpf[b]], writes=[Rt[s]])
                    dma("sp", dst[g * 128:(g + 1) * 128, :], tstg[:, s, 0:n], cx.dsem(17 + s), reads=[Rt[s]])
        cx.barrier()

    if dbg == 1:
        gst.close()
        cx.stack.close()
        return nc

    with ExitStack() as st:
        KT = sb(st, "KT", [128, 2, T], BF16)
        QT = sb(st, "QT", [128, 2, T], BF16)
        GT = sb(st, "GT", [128, 2, T], BF16)
        V = sb(st, "V", [128, 2, NT, 128], BF16)
        Pt = sb(st, "Pt", [128, 3, 2, 512], BF16)
        osb2 = sb(st, "osb2", [128, 2, 512], F32)
        zsb = sb(st, "zsb", [128, 2, 512], F32)
        ob = sb(st, "ob", [128, 4, 512], F32)
        sqb = sb(st, "sqb", [128, 4, 512], BF16)
        lnb = sb(st, "lnb", [128, 4, 512], F32)
        rsb = sb(st, "rsb", [128, 4, 512], F32)
        tmpb = sb(st, "tmpb", [128, 512], F32)
        ystg = sb(st, "ystg", [128, 4, 512], BF16)
        Sall = pst(st, "Sall", [128, 4, 512], F32)
        accO = pst(st, "accO", [128, 2, 512], F32)
        accZ = pst(st, "accZ", [128, 2, 512], F32)
        acc = [accO[:, 0, :], accO[:, 1, :], accZ[:, 0, :], accZ[:, 1, :]]

        RK = [R("K%d" % i) for i in range(2)]
        RQ = [R("Q%d" % i) for i in range(2)]
        RG = [R("G%d" % i) for i in range(2)]
        RV = [R("V%d" % i) for i in range(2)]
        RS = [R("S%d" % i) for i in range(2)]
        RP = [R("P%d" % i) for i in range(3)]
        RA = [R("acc%d" % i) for i in range(4)]
        Ros2 = [R("osb2_%d" % i) for i in range(2)]
        Rzs = [R("zsb%d" % i) for i in range(2)]
        Rob = [R("ob%d" % i) for i in range(4)]
        Rsq = [R("sq%d" % i) for i in range(4)]
        Rln = [R("ln%d" % i) for i in range(4)]
        Rrs = [R("rs%d" % i) for i in range(4)]
        Rtmp = R("tmp")
        Rys = [R("ys%d" % i) for i in range(4)]

        def load_head(h):
            hs = h % 2
            CH = min(T, 2048)
            for c in range(0, T, CH):
                dma("sp", KT[:, hs, c:c + CH], s_kT[h][:, c:c + CH], cx.dsem(21 + hs), writes=[RK[hs]])
                dma("sp", QT[:, hs, c:c + CH], s_qT[h][:, c:c + CH], cx.dsem(23 + hs), writes=[RQ[hs]])
            for t0 in range(0, NT, 8):
                dma("sp", V[:, hs, t0:t0 + 8, :],
                    s_v[t0 * 128:(t0 + 8) * 128, h * 128:(h + 1) * 128].rearrange("(t p) e -> p t e", p=128),
                    cx.dsem(25 + hs), writes=[RV[hs]])
            for c in range(0, T, CH):
                dma("sp", GT[:, hs, c:c + CH], s_gT[h][:, c:c + CH], cx.dsem(27 + hs), writes=[RG[hs]])

        pairs = []
        qbc = 0
        for h in range(4):
            Nq = 256 if h == 0 else 512
            r = Nq // 128
            for qb in range(T // Nq):
                nk = (qb + 1) * r
                kt_lo = 0
                if WIN_THR is not None:
                    while kt_lo < nk - r and SLOPES[h] * (qb * Nq - (kt_lo * 128 + 127)) >= WIN_THR:
                        kt_lo += 1
                for kt in range(kt_lo, nk):
                    pairs.append((h, qb, kt, nk, Nq, r, qbc, kt_lo))
                qbc += 1

        pending = []
        sc = [0]

        def defer(at, fn):
            pending.append((at, len(pending), fn))
            pending.sort(key=lambda t: (t[0], t[1]))

        def epilogue(n, h, qb, Nq, qbc):
            hs = h % 2
            q0 = qb * Nq
            osl = qbc % 4
            op("dve", lambda e: e.tensor_copy(out=osb2[:, :, :Nq], in_=accO[:, :, :Nq]), reads=[RA[0], RA[1]],
               writes=[Ros2[0], Ros2[1]])
            op("act", lambda e: e.activation(out=zsb[:, :, :Nq], in_=accZ[:, :, :Nq], func=AF.Copy),
               reads=[RA[2], RA[3]], writes=[Rzs[0], Rzs[1]])
            for i in range(2):
                op("dve", lambda e: e.reciprocal(out=zsb[:, i, :Nq], in_=zsb[:, i, :Nq]), reads=[Rzs[i]], writes=[Rzs[i]])
                op("dve", lambda e: e.tensor_tensor(out=osb2[:, i, :Nq], in0=osb2[:, i, :Nq], in1=zsb[:, i, :Nq],
                                                    op=ALU.mult), reads=[Ros2[i], Rzs[i]], writes=[Ros2[i]])
            op("dve", lambda e: e.scalar_tensor_tensor(out=ob[:, osl, :Nq], in0=osb2[:, 1, :Nq], scalar=neglam[:],
                                                       in1=osb2[:, 0, :Nq], op0=ALU.mult, op1=ALU.add),
               reads=[Ros2[0], Ros2[1], Rl], writes=[Rob[osl]])
            op("pool", lambda e: e.tensor_tensor(out=sqb[:, osl, :Nq], in0=ob[:, osl, :Nq], in1=ob[:, osl, :Nq],
                                                 op=ALU.mult), reads=[Rob[osl]], writes=[Rsq[osl]])
            box = {}

            def f_pe():
                s = sc[0] % 2
                sc[0] += 1
                box["s"] = s
                op("pe", lambda e: e.matmul(Sall[:, 2 * s, :Nq], lhsT=onesmb[:], rhs=sqb[:, osl, :Nq], start=True,
                                            stop=True), reads=[Rsq[osl], Rc], writes=[RS[s]])

            def f_act():
                s = box["s"]
                op("act", lambda e: e.activation(out=lnb[:, osl, :Nq], in_=Sall[:, 2 * s, :Nq], func=AF.Ln,
                                                 bias=epsc[:], scale=1.0), reads=[RS[s]], writes=[Rln[osl]])
                op("act", lambda e: e.activation(out=rsb[:, osl, :Nq], in_=lnb[:, osl, :Nq], func=AF.Exp, scale=-0.5),
                   reads=[Rln[osl]], writes=[Rrs[osl]])

            def f_dve():
                op("dve", lambda e: e.tensor_tensor(out=tmpb[:, :Nq], in0=ob[:, osl, :Nq], in1=rsb[:, osl, :Nq],
                                                    op=ALU.mult), reads=[Rob[osl], Rrs[osl]], writes=[Rtmp])
                op("dve", lambda e: e.scalar_tensor_tensor(out=ystg[:, osl, :Nq], in0=tmpb[:, :Nq], scalar=wcol_d[:],
                                                           in1=GT[:, hs, q0:q0 + Nq], op0=ALU.mult, op1=ALU.mult),
                   reads=[Rtmp, RG[hs], Rl], writes=[Rys[osl]])
                dma("sp", s_yT[h][:, q0:q0 + Nq], ystg[:, osl, :Nq], cx.dsem(52 + osl), reads=[Rys[osl]])

            defer(n + 13, f_pe)
            defer(n + 13, f_act)
            defer(n + 15, f_dve)

        LA = 1
        NPR = len(pairs)
        load_head(0)
        for n in range(NPR + LA):
            while pending and pending[0][0] <= n:
                pending.pop(0)[2]()
            if n < NPR:
                h, qb, kt, nk, Nq, r, qbc, kt_lo = pairs[n]
                hs = h % 2
                if qb == 0 and kt == kt_lo and h + 1 < 4:
                    cnt_h = sum(1 for p in pairs if p[0] == h)

                    def ld(h=h):
                        while pending:
                            pending.pop(0)[2]()
                        load_head(h + 1)
                    defer(n + min(20, max(1, cnt_h - 2)), ld)
                q0 = qb * Nq
                s = sc[0] % 2
                sc[0] += 1
                sP = n % 3
                jd = kt - r * qb
                cl = 128 * jd if jd > 0 else 0
                diag = jd >= 0
                for i in range(2):
                    op("pe", lambda e: e.matmul(Sall[:, 2 * s + i, cl:Nq],
                                                lhsT=KT[64 * i:64 * i + 64, hs, kt * 128:(kt + 1) * 128],
                                                rhs=QT[64 * i:64 * i + 64, hs, q0 + cl:q0 + Nq], start=True,
                                                stop=not diag),
                       reads=[RK[hs], RQ[hs]], writes=[RS[s]], inc=(i == 1 and not diag))
                if diag:
                    for i in range(2):
                        op("pe", lambda e: e.matmul(Sall[:, 2 * s + i, cl:cl + 128], lhsT=ident[:], rhs=Mtri[:],
                                                    start=False, stop=True),
                           reads=[Rc], writes=[RS[s]], inc=(i == 1))
                jj = r * qb - kt + 3
                op("act", lambda e: e.activation(out=Pt[:, sP, :, cl:Nq], in_=Sall[:, 2 * s:2 * s + 2, cl:Nq],
                                                 func=AF.Exp, bias=bts[h][:, jj:jj + 1], scale=1.0),
                   reads=[RS[s], Rc], writes=[RP[sP]])
            m = n - LA
            if m >= 0:
                h, qb, kt, nk, Nq, r, qbc, kt_lo = pairs[m]
                hs = h % 2
                sP = m % 3
                jd = kt - r * qb
                cl = 128 * jd if jd > 0 else 0
                for i in range(2):
                    op("pe", lambda e: e.matmul(acc[i][:, cl:Nq], lhsT=V[:, hs, kt, :], rhs=Pt[:, sP, i, cl:Nq],
                                                start=(kt == kt_lo), stop=(kt == nk - 1)),
                       reads=[RV[hs], RP[sP]], writes=[RA[i]], inc=False)
                for i in range(2):
                    op("pe", lambda e: e.matmul(acc[2 + i][:, cl:Nq], lhsT=onesb[:], rhs=Pt[:, sP, i, cl:Nq],
                                                start=(kt == kt_lo), stop=(kt == nk - 1)),
                       reads=[RP[sP], Rc], writes=[RA[2 + i]], inc=(i == 1))
                if kt == nk - 1:
                    epilogue(n, h, qb, Nq, qbc)
        while pending:
            pending.pop(0)[2]()
        cx.barrier()

    if dbg == 2:
        gst.close()
        cx.stack.close()
        return nc

    if not FUSE4:
        wo_bf = sb(gst, "wo_bf", [128, 8, D], BF16)
    with ExitStack() as st:
        gqTb = sb(st, "gqTb", [128, 3, 2, 512], BF16)
        gkTb = sb(st, "gkTb", [128, 3, 2, 512], BF16)
        ggTb = sb(st, "ggTb", [128, 3, 4, 512], BF16)
        gvb = sb(st, "gvb", [128, 3, 4, 512], BF16)
        gkb = sb(st, "gkb", [128, 3, 4, 256], BF16)
        gab = sb(st, "gab", [17, 3, 512], BF16)
        wup = sb(st, "wup", [17, 256], BF16)
        wupf = sb(st, "wupf", [17, 256], F32)
        gmask4 = sb(st, "gmask4", [128, 4, 128], F32)
        e1b = sb(st, "e1b", [128, 256], F32)
        ltb = sb(st, "ltb", [128, 2, 256], F32)
        lhi = sb(st, "lhi", [128, 2, 256], BF16)
        llo = sb(st, "llo", [128, 2, 256], BF16)
        ebT = sb(st, "ebT", [128, 2, 2, 128], F32)
        enbT = sb(st, "enbT", [128, 2, 128], F32)
        ksd = sb(st, "ksd", [128, 256], F32)
        qz = sb(st, "qz", [128, 2, 4, 128], BF16)
        kin = sb(st, "kin", [128, 2, 2, 128], BF16)
        kst = sb(st, "kst", [128, 2, 256], BF16)
        aTb = sb(st, "aTb", [128, 4, 128], BF16)
        Sf = sb(st, "Sf", [128, 2, 128], F32)
        Sbf = sb(st, "Sbf", [128, 2, 2, 128], BF16)
        osb = sb(st, "osb", [128, 4, 512], F32)
        sqg = sb(st, "sqg", [128, 2, 512], BF16)
        lng = sb(st, "lng", [128, 512], F32)
        rsg = sb(st, "rsg", [128, 2, 512], F32)
        tmpg = sb(st, "tmpg", [128, 512], F32)
        yg = sb(st, "yg", [128, 8, 4, 128], BF16)
        pA_ = pst(st, "pA", [128, 512], F32)
        pB_ = pst(st, "pB", [128, 512], F32)
        pC_ = pst(st, "pC", [128, 512], F32)
        pD_ = pst(st, "pD", [128, 512], F32)
        pE_ = pst(st, "pE", [128, 512], F32)
        pF_ = pst(st, "pF", [128, 512], F32)
        pG = pst(st, "pG", [128, 512], F32)
        pA = pA_[:, 0:256]
        pB = pB_[:, 0:256].rearrange("p (a b) -> p a b", a=2)
        pC = pC_[:, 0:256]
        pD = pD_[:, :].rearrange("p (a b) -> p a b", a=4)
        pE = pE_[:, :].rearrange("p (a b) -> p a b", a=4)
        pF = pF_[:, :].rearrange("p (a b) -> p a b", a=2)

        Rblk = [[R("gblk%d_%d" % (k, s)) for s in range(3)] for k in range(6)]
        Rg0 = R("g0")
        RA, RB, RC, RD, RE, RF, RG_ = [R("pg%d" % i) for i in range(7)]
        Re1 = R("e1")
        Rlt = [R("lt%d" % i) for i in range(2)]
        Rlh = [R("lh%d" % i) for i in range(2)]
        Rll = [R("ll%d" % i) for i in range(2)]
        Reb = [R("eb%d" % i) for i in range(2)]
        Renb = R("enb")
        Rksd = R("ksd")
        Rqin = [R("qin%d" % i) for i in range(2)]
        Rkin = [R("kin%d" % i) for i in range(2)]
        Rkst = [R("kst%d" % i) for i in range(2)]
        RaT = R("aT")
        RSf = R("Sf")
        RSf4 = [R("Sf%d" % i) for i in range(4)]
        RSb = [R("Sb%d" % i) for i in range(2)]
        Ros = [R("os%d" % i) for i in range(4)]
        Rsqg = [R("sqg%d" % i) for i in range(2)]
        Rlng = R("lng")
        Rrsg = [R("rsg%d" % i) for i in range(2)]
        Rtmpg = R("tmpg")
        Ryg = [R("yg%d" % i) for i in range(8)]

        op("pool", lambda e: e.memset(gab[:], 1.0), writes=[Rblk[5][0], Rblk[5][1], Rblk[5][2]])
        op("pool", lambda e: e.memset(Sf[:], 0.0), writes=[RSf])
        op("pool", lambda e: e.memset(Sbf[:], 0.0), writes=[RSb[0], RSb[1]])
        op("pool", lambda e: e.memset(qz[:], 0.0), writes=[Rqin[0], Rqin[1]])
        for hh in range(4):
            op("pool", lambda e: e.tensor_copy(out=gmask4[:, hh, :], in_=gmask[:]), reads=[Rc], writes=[Rg0])
        Rwupf = R("wupf")
        dma("sp", wupf[0:16, :], w_gate_up, cx.dsem(0), writes=[Rwupf])
        dma("sp", wupf[16:17, :], b_gate, cx.dsem(0), writes=[Rwupf])
        op("pool", lambda e: e.tensor_copy(out=wup[:], in_=wupf[:]), reads=[Rwupf], writes=[Rg0])

        def gload(j):
            s = j % 3
            c = slice(j * 512, (j + 1) * 512)
            dma("sp", gab[0:16, s, :], s_gaT[:, c], cx.dsem(31 + s), writes=[Rblk[5][s]])
            dma("sp", gqTb[:, s, :, :], s_gqT[:, :, c].rearrange("c p t -> p c t"), cx.dsem(34 + s), writes=[Rblk[0][s]])
            dma("sp", gkTb[:, s, :, :], s_gkT[:, :, c].rearrange("c p t -> p c t"), cx.dsem(37 + s), writes=[Rblk[1][s]])
            dma("sp", gkb[:, s, :, :], s_gk[c, :].rearrange("(t p) e -> p t e", p=128), cx.dsem(40 + s),
                writes=[Rblk[4][s]])
            dma("sp", gvb[:, s, :, :], s_gv[c, :].rearrange("(t p) e -> p t e", p=128), cx.dsem(43 + s),
                writes=[Rblk[3][s]])
            dma("sp", ggTb[:, s, :, :], s_ggT[:, :, c].rearrange("c p t -> p c t"), cx.dsem(46 + s), writes=[Rblk[2][s]])

        def s1a(t):
            j, tt = divmod(t, 4)
            s = j % 3
            c0 = tt * 128
            ls = t % 2
            op("pe", lambda e: e.matmul(pA[:, :], lhsT=gab[0:17, s, c0:c0 + 128], rhs=wup[0:17, :], start=True, stop=True),
               reads=[Rblk[5][s], Rg0], writes=[RA])
            op("act", lambda e: e.activation(out=e1b[:], in_=pA[:, :], func=AF.Exp, scale=-1.0), reads=[RA], writes=[Re1])
            op("act", lambda e: e.activation(out=ltb[:, ls, :], in_=e1b[:], func=AF.Ln, bias=onesf[:, 0:1], scale=1.0),
               reads=[Re1, Rc], writes=[Rlt[ls]])
            op("dve", lambda e: e.tensor_copy(out=lhi[:, ls, :], in_=ltb[:, ls, :]), reads=[Rlt[ls]], writes=[Rlh[ls]])
            op("pool", lambda e: e.tensor_tensor(out=llo[:, ls, :], in0=ltb[:, ls, :], in1=lhi[:, ls, :],
                                                 op=ALU.subtract), reads=[Rlt[ls], Rlh[ls]], writes=[Rll[ls]])

        def s1b(t):
            j, tt = divmod(t, 4)
            s = j % 3
            c0 = tt * 128
            ls = t % 2
            for p in range(2):
                op("pe", lambda e: e.matmul(pB[:, p, :], lhsT=lhi[:, ls, p * 128:(p + 1) * 128], rhs=Ucsb[:],
                                            start=True, stop=False), reads=[Rlh[ls], Rc], writes=[RB], inc=False)
                op("pe", lambda e: e.matmul(pB[:, p, :], lhsT=llo[:, ls, p * 128:(p + 1) * 128], rhs=Ucsb[:],
                                            start=False, stop=True), reads=[Rll[ls], Rc], writes=[RB])
            op("pe", lambda e: e.matmul(pC[:, :], lhsT=Ustb[:], rhs=lhi[:, ls, :], start=True, stop=False),
               reads=[Rlh[ls], Rc], writes=[RC], inc=False)
            op("pe", lambda e: e.matmul(pC[:, :], lhsT=Ustb[:], rhs=llo[:, ls, :], start=False, stop=True),
               reads=[Rll[ls], Rc], writes=[RC])
            op("act", lambda e: e.activation(out=ebT[:, ls, :, :], in_=pB[:, :, :], func=AF.Exp), reads=[RB], writes=[Reb[ls]])
            op("act", lambda e: e.activation(out=enbT[:, :, :], in_=pB[:, :, :], func=AF.Exp, scale=-1.0),
               reads=[RB], writes=[Renb])
            op("act", lambda e: e.activation(out=ksd[:], in_=pC[:, :], func=AF.Exp), reads=[RC], writes=[Rksd])
            for h2 in range(2):
                r0 = 64 * h2
                op("dve", lambda e: e.tensor_tensor(out=qz[r0:r0 + 64, ls, h2::2, :],
                                                    in0=gqTb[r0:r0 + 64, s, :, c0:c0 + 128],
                                                    in1=ebT[r0:r0 + 64, ls, :, :], op=ALU.mult),
                   reads=[Rblk[0][s], Reb[ls]], writes=[Rqin[ls]])
            op("pool", lambda e: e.tensor_tensor(out=kin[:, ls, :, :], in0=gkTb[:, s, :, c0:c0 + 128], in1=enbT[:, :, :],
                                                 op=ALU.mult), reads=[Rblk[1][s], Renb], writes=[Rkin[ls]])
            op("pool", lambda e: e.tensor_tensor(out=kst[:, ls, :], in0=gkb[:, s, tt, :], in1=ksd[:], op=ALU.mult),
               reads=[Rblk[4][s], Rksd], writes=[Rkst[ls]])

        def s2a(t):
            j, tt = divmod(t, 4)
            s = j % 3
            ls = t % 2
            cur, nxt = t % 2, (t + 1) % 2
            for hh in range(4):
                p, r0 = hh // 2, 64 * (hh % 2)
                op("pe", lambda e: e.matmul(pD[:, hh, :], lhsT=kin[:, ls, p, :], rhs=qz[:, ls, hh, :],
                                            start=True, stop=True), reads=[Rkin[ls], Rqin[ls]], writes=[RD])
            for p in range(2):
                op("pe", lambda e: e.matmul(pF[:, p, :], lhsT=kst[:, ls, p * 128:(p + 1) * 128],
                                            rhs=gvb[:, s, tt, p * 256:(p + 1) * 256], start=True, stop=True),
                   reads=[Rkst[ls], Rblk[3][s]], writes=[RF])
            op("dve", lambda e: e.tensor_tensor(out=aTb[:, :, :], in0=pD[:, :, :], in1=gmask4[:, :, :], op=ALU.mult),
               reads=[RD, Rg0], writes=[RaT])
            for hh in range(4):
                p, r0 = hh // 2, 64 * (hh % 2)
                op("pe", lambda e: e.matmul(pE[:, hh, :], lhsT=gvb[:, s, tt, hh * 128:(hh + 1) * 128], rhs=aTb[:, hh, :],
                                            start=True, stop=False), reads=[Rblk[3][s], RaT], writes=[RE])
                op("pe", lambda e: e.matmul(pE[:, hh, :], lhsT=Sbf[:, cur, p, :], rhs=qz[:, ls, hh, :],
                                            start=False, stop=True), reads=[RSb[cur], Rqin[ls]], writes=[RE])
            for hh in range(4):
                p, h2 = hh // 2, hh % 2
                r0 = 64 * h2
                op("dve", lambda e: e.scalar_tensor_tensor(out=Sf[r0:r0 + 64, p, :], in0=Sf[r0:r0 + 64, p, :],
                                                           scalar=ebT[r0:r0 + 64, ls, p, 127:128],
                                                           in1=pF[r0:r0 + 64, p, h2 * 128:(h2 + 1) * 128],
                                                           op0=ALU.mult, op1=ALU.add),
                   reads=[RSf, RSf4[hh], Reb[ls], RF], writes=[RSf4[hh]])
            op("pool", lambda e: e.tensor_copy(out=Sbf[:, nxt, :, :], in_=Sf[:, :, :]), reads=[RSf] + RSf4,
               writes=[RSb[nxt]])
            o4 = t % 4
            op("act", lambda e: e.activation(out=osb[:, o4, :], in_=pE_[:, :], func=AF.Copy), reads=[RE], writes=[Ros[o4]])

        def s2b(t):
            o4, s2 = t % 4, t % 2
            op("pool", lambda e: e.tensor_tensor(out=sqg[:, s2, :], in0=osb[:, o4, :], in1=osb[:, o4, :], op=ALU.mult),
               reads=[Ros[o4]], writes=[Rsqg[s2]])

        def s2c(t):
            s2 = t % 2
            op("pe", lambda e: e.matmul(pG[:, :], lhsT=onesmb[:], rhs=sqg[:, s2, :], start=True, stop=True),
               reads=[Rsqg[s2], Rc], writes=[RG_])
            op("act", lambda e: e.activation(out=lng[:], in_=pG[:, :], func=AF.Ln, bias=epsc[:], scale=1.0),
               reads=[RG_], writes=[Rlng])
            op("act", lambda e: e.activation(out=rsg[:, s2, :], in_=lng[:], func=AF.Exp, scale=-0.5),
               reads=[Rlng], writes=[Rrsg[s2]])

        def s2d(t):
            j, tt = divmod(t, 4)
            s = j % 3
            c0 = tt * 128
            o4, s2 = t % 4, t % 2
            op("dve", lambda e: e.tensor_tensor(out=tmpg[:], in0=osb[:, o4, :], in1=rsg[:, s2, :], op=ALU.mult),
               reads=[Ros[o4], Rrsg[s2]], writes=[Rtmpg])
            s8 = t % 8
            op("dve", lambda e: e.scalar_tensor_tensor(out=yg[:, s8, :, :], in0=tmpg[:], scalar=wcol_g[:],
                                                       in1=ggTb[:, s, :, c0:c0 + 128], op0=ALU.mult, op1=ALU.mult),
               reads=[Rtmpg, Rblk[2][s], Rl], writes=[Ryg[s8]])
            if not FUSE4:
                dma("sp", s_yT[4:8, :, t * 128:(t + 1) * 128].rearrange("h p t -> p h t"), yg[:, s8, :, :],
                    cx.dsem(49 + s2), reads=[Ryg[s8]])

        if not FUSE4:
            wst2g = sb(st, "wst2g", [128, 2, D], F32)
            Rwo = [R("wo%d" % k) for k in range(8)]
            Rwst2 = [R("wst2_%d" % k) for k in range(2)]

            def wchunk(kc):
                s = kc % 2
                dma("sp", wst2g[:, s, :], w_out[kc * 128:(kc + 1) * 128, :], cx.dsem(1 + s), writes=[Rwst2[s]])
                op("pool", lambda e: e.tensor_copy(out=wo_bf[:, kc, :], in_=wst2g[:, s, :]), reads=[Rwst2[s]],
                   writes=[Rwo[kc]])
        if FUSE4:
            wo_bf = sb(st, "wo_bf", [128, 8, D], BF16)
            wst2 = sb(st, "wst2", [128, 2, D], F32)
            yTb = sb(st, "yTb", [128, 2, 4, 512], BF16)
            xs4 = sb(st, "xs4", [128, 4, D], F32)
            zb = sb(st, "zb", [128, 2, D], F32)
            junk4 = sb(st, "junk4", [128, D], F32)
            stat4 = sb(st, "stat4", [128, 2, 4], F32)
            ost = sb(st, "ost", [128, 2, D], F32)
            po1 = pst(st, "po1", [128, 512], F32)
            Rwo = [R("wo%d" % k) for k in range(8)]
            Rwst2 = [R("wst2_%d" % k) for k in range(2)]
            for kc in range(8):
                s = kc % 2
                dma("sp", wst2[:, s, :], w_out[kc * 128:(kc + 1) * 128, :], cx.dsem(1 + s), writes=[Rwst2[s]])
                op("dve" if kc % 2 == 0 else "pool",
                   lambda e: e.tensor_copy(out=wo_bf[:, kc, :], in_=wst2[:, s, :]), reads=[Rwst2[s]], writes=[Rwo[kc]])
            RyT = [R("yT%d" % i) for i in range(2)]
            Rx4 = [R("x4_%d" % i) for i in range(4)]
            Rpo1 = R("po1")
            Rz = [R("z%d" % i) for i in range(2)]
            Rj4 = R("junk4")
            Rs4 = [R("s4_%d" % i) for i in range(2)]
            Ro = [R("ost%d" % i) for i in range(2)]

            def yload(j):
                s = j % 2
                dma("sp", yTb[:, s, :, :], s_yT[0:4, :, j * 512:(j + 1) * 512].rearrange("c p t -> p c t"),
                    cx.dsem(21 + s), writes=[RyT[s]])

            def xload4(g):
                xsl = g % 4
                dma("sp", xs4[:, xsl, :], x[g * 128:(g + 1) * 128, :], cx.dsem(3 + xsl), writes=[Rx4[xsl]])

            def p4tile(g):
                j, tt = divmod(g, 4)
                sl = j % 2
                xsl = g % 4
                z2 = g % 2
                s8 = g % 8
                if tt == 0 and j + 1 < NB:
                    yload(j + 1)
                if g + 2 < NT:
                    xload4(g + 2)
                if P4L < 2:
                    return
                for nh in range(2):
                    for kc in range(8):
                        if kc < 4:
                            lh = yTb[:, sl, kc, tt * 128:(tt + 1) * 128]
                            rd = [RyT[sl], Rwo[kc]]
                        else:
                            lh = yg[:, s8, kc - 4, :]
                            rd = [Ryg[s8], Rwo[kc]]
                        op("pe", lambda e: e.matmul(po1[:, :], lhsT=lh, rhs=wo_bf[:, kc, nh * 512:(nh + 1) * 512],
                                                    start=(kc == 0), stop=(kc == 7)), reads=rd, writes=[Rpo1],
                           inc=(kc == 7))
                    if P4L < 3:
                        continue
                    op("dve", lambda e: e.tensor_tensor(out=zb[:, z2, nh * 512:(nh + 1) * 512], in0=po1[:, :],
                                                        in1=xs4[:, xsl, nh * 512:(nh + 1) * 512], op=ALU.add),
                       reads=[Rpo1, Rx4[xsl]], writes=[Rz[z2]])
                if P4L < 4:
                    return
                op("act", lambda e: e.activation(out=junk4[:], in_=zb[:, z2, :], func=AF.Square,
                                                 accum_out=stat4[:, z2, 0:1]), reads=[Rz[z2]], writes=[Rj4, Rs4[z2]])
                op("act", lambda e: e.activation(out=stat4[:, z2, 1:2], in_=stat4[:, z2, 0:1], func=AF.Ln, scale=1.0 / D,
                                                 bias=epsc[:]), reads=[Rs4[z2]], writes=[Rs4[z2]])
                op("act", lambda e: e.activation(out=stat4[:, z2, 2:3], in_=stat4[:, z2, 1:2], func=AF.Exp, scale=-0.5),
                   reads=[Rs4[z2]], writes=[Rs4[z2]])
                if P4L < 5:
                    return
                op("dve", lambda e: e.scalar_tensor_tensor(out=ost[:, z2, :], in0=zb[:, z2, :], scalar=stat4[:, z2, 2:3],
                                                           in1=fnw_b[:], op0=ALU.mult, op1=ALU.mult),
                   reads=[Rz[z2], Rs4[z2], Rl], writes=[Ro[z2]])
                if P4L < 6:
                    return
                dma("sp", out[g * 128:(g + 1) * 128, :], ost[:, z2, :], cx.dsem(56 + z2), reads=[Ro[z2]])

            yload(0)
            xload4(0)
            xload4(1)

        gload(0)
        for it in range(NT + (7 if FUSE4 else 5)):
            if it % 4 == 1 and (it // 4) + 1 < NB:
                gload(it // 4 + 1)
            if not FUSE4 and 2 <= it < 10:
                wchunk(it - 2)
            if it < NT and GST >= 1:
                s1a(it)
            if 0 <= it - 1 < NT and GST >= 2:
                s1b(it - 1)
            if 0 <= it - 2 < NT and GST >= 3:
                s2a(it - 2)
            if 0 <= it - 3 < NT and GST >= 4:
                s2b(it - 3)
            if 0 <= it - 4 < NT and GST >= 5:
                s2c(it - 4)
            if 0 <= it - 5 < NT and GST >= 6:
                s2d(it - 5)
            if FUSE4 and 0 <= it - 6 < NT and P4L >= 1:
                p4tile(it - 6)
        cx.barrier()

    if dbg == 3:
        gst.close()
        cx.stack.close()
        return nc

    with ExitStack() as st:
        if FUSE4:
            gst.close()
            cx.stack.close()
            return nc
        yTb = sb(st, "yTb", [128, 2, 8, 512], BF16)
        xs4 = sb(st, "xs4", [128, 4, D], F32)
        zb = sb(st, "zb", [128, 2, D], F32)
        junk4 = sb(st, "junk4", [128, D], F32)
        stat4 = sb(st, "stat4", [128, 2, 4], F32)
        ost = sb(st, "ost", [128, 2, D], F32)
        po = [pst(st, "po%d" % i, [128, 512], F32) for i in range(4)]

        RyT = [R("yT%d" % i) for i in range(2)]
        Rx4 = [R("x4_%d" % i) for i in range(4)]
        Rpo = [R("po%d" % i) for i in range(4)]
        Rz = [R("z%d" % i) for i in range(2)]
        Rj4 = R("junk4")
        Rs4 = [R("s4_%d" % i) for i in range(2)]
        Ro = [R("ost%d" % i) for i in range(2)]

        def yload(j):
            s = j % 2
            dma("sp", yTb[:, s, :, :], s_yT[:, :, j * 512:(j + 1) * 512].rearrange("c p t -> p c t"), cx.dsem(21 + s),
                writes=[RyT[s]])

        def xload4(g):
            xsl = g % 4
            dma("sp", xs4[:, xsl, :], x[g * 128:(g + 1) * 128, :], cx.dsem(3 + xsl), writes=[Rx4[xsl]])

        fin4 = [None]
        yload(0)
        xload4(0)
        xload4(1)
        for j in range(NB):
            sl = j % 2
            if j + 1 < NB:
                yload(j + 1)
            for tt in range(4):
                g = 4 * j + tt
                xsl = g % 4
                z2 = g % 2
                if g + 2 < NT:
                    xload4(g + 2)
                for nh in range(2):
                    b = 2 * z2 + nh
                    for kc in range(8):
                        op("pe", lambda e: e.matmul(po[b][:, :], lhsT=yTb[:, sl, kc, tt * 128:(tt + 1) * 128],
                                                    rhs=wo_bf[:, kc, nh * 512:(nh + 1) * 512], start=(kc == 0),
                                                    stop=(kc == 7)), reads=[RyT[sl], Rwo[kc]], writes=[Rpo[b]],
                           inc=(kc == 7))
                    op("dve", lambda e: e.tensor_tensor(out=zb[:, z2, nh * 512:(nh + 1) * 512], in0=po[b][:, :],
                                                        in1=xs4[:, xsl, nh * 512:(nh + 1) * 512], op=ALU.add),
                       reads=[Rpo[b], Rx4[xsl]], writes=[Rz[z2]])
                if fin4[0] is not None:
                    fin4[0]()
                op("act", lambda e: e.activation(out=junk4[:], in_=zb[:, z2, :], func=AF.Square,
                                                 accum_out=stat4[:, z2, 0:1]), reads=[Rz[z2]], writes=[Rj4, Rs4[z2]])
                op("act", lambda e: e.activation(out=stat4[:, z2, 1:2], in_=stat4[:, z2, 0:1], func=AF.Ln, scale=1.0 / D,
                                                 bias=epsc[:]), reads=[Rs4[z2]], writes=[Rs4[z2]])
                op("act", lambda e: e.activation(out=stat4[:, z2, 2:3], in_=stat4[:, z2, 1:2], func=AF.Exp, scale=-0.5),
                   reads=[Rs4[z2]], writes=[Rs4[z2]])

                def fin(g=g, z2=z2):
                    op("dve", lambda e: e.scalar_tensor_tensor(out=ost[:, z2, :], in0=zb[:, z2, :],
                                                               scalar=stat4[:, z2, 2:3], in1=fnw_b[:], op0=ALU.mult,
                                                               op1=ALU.mult),
                       reads=[Rz[z2], Rs4[z2], Rl], writes=[Ro[z2]])
                    dma("sp", out[g * 128:(g + 1) * 128, :], ost[:, z2, :], cx.dsem(7 + z2), reads=[Ro[z2]])
                fin4[0] = fin
        fin4[0]()
        cx.barrier()

    gst.close()
    cx.stack.close()
    return nc


_NC_CACHE = {}


def kernel(**inputs):
    xin = np.asarray(inputs["x"], dtype=np.float32)
    B, T, _ = xin.shape
    if T not in _NC_CACHE:
        _NC_CACHE[T] = build(T)
    nc = _NC_CACHE[T]

    def a2(name, shape):
        return np.ascontiguousarray(np.asarray(inputs[name], dtype=np.float32).reshape(shape))

    shared = dict(
        norm_w=a2("norm_w", (1, D)), w_in=a2("w_in", (D, INC)), w_gate_up=a2("w_gate_up", (16, 256)),
        b_gate=a2("b_gate", (1, 256)), lambda_q1=a2("lambda_q1", (1, 64)), lambda_k1=a2("lambda_k1", (1, 64)),
        lambda_q2=a2("lambda_q2", (1, 64)), lambda_k2=a2("lambda_k2", (1, 64)),
        diff_subln_w=a2("diff_subln_w", (1, 128)), gla_norm_w=a2("gla_norm_w", (1, 128)),
        w_out=a2("w_out", (D, D)), final_norm_w=a2("final_norm_w", (1, D)))
    in_maps = []
    for b in range(B):
        m = dict(shared)
        m["x"] = np.ascontiguousarray(xin[b])
        in_maps.append(m)
    res = run_bass_kernel_spmd(nc, in_maps, core_ids=list(range(B)))
    return np.stack([np.asarray(r["out"], dtype=np.float32) for r in res.results], axis=0)
```

```python
import math
from contextlib import ExitStack

import numpy as np
import concourse.bass as bass
import concourse.mybir as mybir
from concourse.bass_utils import run_bass_kernel_spmd

F32 = mybir.dt.float32
BF16 = mybir.dt.bfloat16
I32 = mybir.dt.int32
AF = mybir.ActivationFunctionType
ALU = mybir.AluOpType

D = 1024
INC = 3600
C_DQ, C_DK, C_DV, C_DG, C_GQ, C_GK, C_GV, C_GG, C_GA = 0, 512, 1024, 1536, 2048, 2304, 2560, 3072, 3584
SLOPES = [2.0 ** (-8.0 * (h + 1) / 4) for h in range(4)]
LAM_INIT = 0.8 - 0.6 * math.exp(-0.3 * 0)
EPS = 1e-6
SEQ = 8192
import os
GST = int(os.environ.get('GST', '6'))
WIN_THR = 64.0
P4L = int(os.environ.get('P4L', '9'))
FUSE4 = False


class R:
    __slots__ = ("name", "w", "r")

    def __init__(self, name):
        self.name = name
        self.w = {}
        self.r = {}


class Ctx:
    def __init__(self, nc):
        self.nc = nc
        self.E = dict(pe=nc.tensor, act=nc.scalar, dve=nc.vector, pool=nc.gpsimd, sp=nc.sync)
        self.sem = {}
        self.cnt = {}
        self.seen = {e: {} for e in self.E}
        self.stack = ExitStack()
        for e in self.E:
            self.newsem(e)
        self.ndsem = 0

    def newsem(self, key):
        s = self.stack.enter_context(self.nc.semaphore("s_" + key))
        self.sem[key] = s
        self.cnt[key] = 0
        return key

    def dsem(self, i):
        key = "d%d" % i
        if key not in self.sem:
            self.newsem(key)
        return key

    def _wait(self, e, deps):
        for k, v in deps.items():
            if v <= 0:
                continue
            if self.seen[e].get(k, 0) < v:
                self.E[e].wait_ge(self.sem[k], v)
                self.seen[e][k] = v

    def _deps(self, e, reads, writes, skip=None):
        deps = {}
        for r in reads:
            for k, v in r.w.items():
                if k == e and e == "pe":
                    continue
                if k == skip:
                    continue
                if deps.get(k, 0) < v:
                    deps[k] = v
        for w in writes:
            dd = w.r if w.r else w.w
            for k, v in dd.items():
                if k == skip or (k == e and e == "pe"):
                    continue
                if deps.get(k, 0) < v:
                    deps[k] = v
        return deps

    def op(self, e, fn, reads=(), writes=(), inc=True):
        self._wait(e, self._deps(e, reads, writes))
        ins = fn(self.E[e])
        if inc:
            self.cnt[e] += 1
            ins.then_inc(self.sem[e], 1)
            c = self.cnt[e]
        else:
            c = self.cnt[e] + 1
        for r in reads:
            r.r[e] = c
        for w in writes:
            w.w = {e: c}
            w.r = {}
        return ins

    def dma(self, q, out, in_, sem, reads=(), writes=()):
        self._wait(q, self._deps(q, reads, writes, skip=sem))
        ins = self.E[q].dma_start(out=out, in_=in_)
        self.cnt[sem] += 16
        ins.then_inc(self.sem[sem], 16)
        c = self.cnt[sem]
        for r in reads:
            r.r[sem] = c
        for w in writes:
            if sem in w.w and not w.r:
                w.w[sem] = c
            else:
                w.w = {sem: c}
            w.r = {}
        return ins

    def barrier(self):
        targets = dict(self.cnt)
        for e in self.E:
            self._wait(e, {k: v for k, v in targets.items() if k != e})


def build(T=SEQ, dbg=False):
    nc = bass.Bass("TRN2", target_bir_lowering=False)
    NB = T // 512
    NT = T // 128
    kind_s = "ExternalOutput" if dbg else "Internal"

    def dram_in(name, shape):
        return nc.dram_tensor(name, shape, F32, kind="ExternalInput").ap()

    x = dram_in("x", [T, D])
    norm_w = dram_in("norm_w", [1, D])
    w_in = dram_in("w_in", [D, INC])
    w_gate_up = dram_in("w_gate_up", [16, 256])
    b_gate = dram_in("b_gate", [1, 256])
    lq1 = dram_in("lambda_q1", [1, 64])
    lk1 = dram_in("lambda_k1", [1, 64])
    lq2 = dram_in("lambda_q2", [1, 64])
    lk2 = dram_in("lambda_k2", [1, 64])
    diff_subln_w = dram_in("diff_subln_w", [1, 128])
    gla_norm_w = dram_in("gla_norm_w", [1, 128])
    w_out = dram_in("w_out", [D, D])
    final_norm_w = dram_in("final_norm_w", [1, D])
    out = nc.dram_tensor("out", [T, D], F32, kind="ExternalOutput").ap()

    def scr(name, shape, dt=BF16):
        return nc.dram_tensor(name, shape, dt, kind=kind_s).ap()

    s_qT = scr("s_qT", [4, 128, T])
    s_kT = scr("s_kT", [4, 128, T])
    s_gT = scr("s_gT", [4, 128, T])
    s_gqT = scr("s_gqT", [2, 128, T])
    s_gkT = scr("s_gkT", [2, 128, T])
    s_ggT = scr("s_ggT", [4, 128, T])
    s_gaT = scr("s_gaT", [16, T])
    s_v = scr("s_v", [T, 512])
    s_gv = scr("s_gv", [T, 512])
    s_gk = scr("s_gk", [T, 256])
    s_yT = scr("s_yT", [8, 128, T])

    cx = Ctx(nc)
    op, dma = cx.op, cx.dma

    gst = ExitStack()

    def sb(st, name, shape, dt):
        return st.enter_context(nc.sbuf_tensor(name, shape, dt))

    def pst(st, name, shape, dt=F32):
        return st.enter_context(nc.psum_tensor(name, shape, dt))

    ident = sb(gst, "ident", [128, 128], BF16)
    onesb = sb(gst, "onesb", [128, 128], BF16)
    onesf = sb(gst, "onesf", [128, 128], F32)
    onesm = sb(gst, "onesm", [128, 128], F32)
    negs = sb(gst, "negs", [128, 128], F32)
    Ucs = sb(gst, "Ucs", [128, 128], F32)
    Ust = sb(gst, "Ust", [128, 128], F32)
    gmask = sb(gst, "gmask", [128, 128], F32)
    onesmb = sb(gst, "onesmb", [128, 128], BF16)
    negbig = sb(gst, "negbig", [128, 128], BF16)
    Mtri = sb(gst, "Mtri", [128, 128], BF16)
    Ucsb = sb(gst, "Ucsb", [128, 128], BF16)
    Ustb = sb(gst, "Ustb", [128, 128], BF16)
    iot = sb(gst, "iot", [128, NT + 4], I32)
    bts = [sb(gst, "bt%d" % h, [128, NT + 4], F32) for h in range(4)]
    nw_b = sb(gst, "nw_b", [128, D], F32)
    fnw_b = sb(gst, "fnw_b", [128, D], F32)
    lam4 = sb(gst, "lam4", [128, 4, 64], F32)
    lamp = sb(gst, "lamp", [128, 2, 64], F32)
    lams = sb(gst, "lams", [128, 8], F32)
    neglam = sb(gst, "neglam", [128, 1], F32)
    wcol_d = sb(gst, "wcol_d", [128, 1], F32)
    wcol_g = sb(gst, "wcol_g", [128, 1], F32)
    epsc = sb(gst, "epsc", [128, 1], F32)
    Rc = R("consts")
    Rl = R("lam")

    P = [("pool", lambda e: e.memset(onesb[:], 1.0)),
         ("pool", lambda e: e.memset(onesf[:], 1.0)),
         ("pool", lambda e: e.memset(onesm[:], 1.0 / 128)),
         ("pool", lambda e: e.memset(negs[:], -1.0 / 16)),
         ("pool", lambda e: e.memset(epsc[:], EPS))]
    for e, f in P:
        op(e, f, writes=[Rc])
    op("pool", lambda e: e.affine_select(out=ident[:], in_=onesb[:], pattern=[[1, 128]], compare_op=ALU.is_equal,
                                          fill=0.0, base=0, channel_multiplier=-1), reads=[Rc], writes=[Rc])
    op("pool", lambda e: e.affine_select(out=Ucs[:], in_=negs[:], pattern=[[1, 128]], compare_op=ALU.is_ge,
                                          fill=0.0, base=0, channel_multiplier=-1), reads=[Rc], writes=[Rc])
    op("pool", lambda e: e.affine_select(out=Ust[:], in_=negs[:], pattern=[[-1, 128]], compare_op=ALU.is_ge,
                                          fill=0.0, base=-1, channel_multiplier=1), reads=[Rc], writes=[Rc])
    op("pool", lambda e: e.affine_select(out=gmask[:], in_=onesf[:], pattern=[[1, 128]], compare_op=ALU.is_ge,
                                          fill=0.0, base=0, channel_multiplier=-1), reads=[Rc], writes=[Rc])
    op("pool", lambda e: e.iota(iot[:], pattern=[[-128, NT + 4]], base=384, channel_multiplier=1),
       reads=[Rc], writes=[Rc])
    op("pool", lambda e: e.tensor_copy(out=onesmb[:], in_=onesm[:]), reads=[Rc], writes=[Rc])
    op("pool", lambda e: e.memset(negbig[:], -30000.0), writes=[Rc])
    op("pool", lambda e: e.affine_select(out=Mtri[:], in_=negbig[:], pattern=[[-1, 128]], compare_op=ALU.is_ge,
                                          fill=0.0, base=-1, channel_multiplier=1), reads=[Rc], writes=[Rc])
    op("pool", lambda e: e.tensor_copy(out=Ucsb[:], in_=Ucs[:]), reads=[Rc], writes=[Rc])
    op("pool", lambda e: e.tensor_copy(out=Ustb[:], in_=Ust[:]), reads=[Rc], writes=[Rc])
    for h in range(4):
        op("dve", lambda e, h=h: e.tensor_scalar(out=bts[h][:], in0=iot[:], scalar1=SLOPES[h], scalar2=None,
                                                op0=ALU.mult), reads=[Rc], writes=[Rc])
    d0 = cx.dsem(0)
    dma("sp", nw_b[:], norm_w.partition_broadcast(128), d0, writes=[Rl])
    dma("sp", fnw_b[:], final_norm_w.partition_broadcast(128), d0, writes=[Rl])
    for i, a in enumerate((lq1, lk1, lq2, lk2)):
        dma("sp", lam4[:, i, :], a.partition_broadcast(128), d0, writes=[Rl])
    dma("sp", wcol_d[:], diff_subln_w.rearrange("o e -> e o"), d0, writes=[Rl])
    dma("sp", wcol_g[:], gla_norm_w.rearrange("o e -> e o"), d0, writes=[Rl])
    op("dve", lambda e: e.tensor_tensor(out=lamp[:, 0, :], in0=lam4[:, 0, :], in1=lam4[:, 1, :], op=ALU.mult),
       reads=[Rl], writes=[Rl])
    op("dve", lambda e: e.tensor_tensor(out=lamp[:, 1, :], in0=lam4[:, 2, :], in1=lam4[:, 3, :], op=ALU.mult),
       reads=[Rl], writes=[Rl])
    op("act", lambda e: e.activation(out=lam4[:, 0, :], in_=lamp[:, 0, :], func=AF.Identity, accum_out=lams[:, 0:1]),
       reads=[Rl], writes=[Rl])
    op("act", lambda e: e.activation(out=lam4[:, 1, :], in_=lamp[:, 1, :], func=AF.Identity, accum_out=lams[:, 1:2]),
       reads=[Rl], writes=[Rl])
    op("act", lambda e: e.activation(out=lams[:, 2:4], in_=lams[:, 0:2], func=AF.Exp), reads=[Rl], writes=[Rl])
    op("dve", lambda e: e.tensor_tensor(out=lams[:, 4:5], in0=lams[:, 2:3], in1=lams[:, 3:4], op=ALU.subtract),
       reads=[Rl], writes=[Rl])
    op("dve", lambda e: e.tensor_scalar(out=neglam[:], in0=lams[:, 4:5], scalar1=LAM_INIT, scalar2=-1.0,
                                        op0=ALU.add, op1=ALU.mult), reads=[Rl], writes=[Rl])
    op("dve", lambda e: e.tensor_scalar(out=wcol_d[:], in0=wcol_d[:], scalar1=1.0 - LAM_INIT, scalar2=None,
                                        op0=ALU.mult), reads=[Rl], writes=[Rl])
    cx.barrier()

    with ExitStack() as st:
        w_bf = sb(st, "w_bf", [128, 8, INC], BF16)
        wst = sb(st, "wst", [128, 2, INC], F32)
        xs = sb(st, "xs", [128, 4, D], F32)
        junk = sb(st, "junk", [128, D], F32)
        hb = sb(st, "hb", [128, 4, D], BF16)
        hT = sb(st, "hT", [128, 2, 8, 512], BF16)
        fstg = sb(st, "fstg", [128, 8, 512], BF16)
        tstg = sb(st, "tstg", [128, 4, 512], BF16)
        gastg = sb(st, "gastg", [16, 2, 512], BF16)
        stat = sb(st, "stat", [128, 4, 4], F32)
        tp = [pst(st, "tp%d" % i, [128, 8, 128], BF16) for i in range(2)]
        pf = [pst(st, "pf%d" % i, [128, 512], F32) for i in range(6)]

        Rw = [R("w%d" % k) for k in range(8)]
        Rwst = [R("wst%d" % k) for k in range(2)]
        Rx = [R("x%d" % i) for i in range(4)]
        for g in range(4):
            dma("sp", xs[:, g, :], x[g * 128:(g + 1) * 128, :], cx.dsem(3 + g), writes=[Rx[g]])
        for kc in range(8):
            s = kc % 2
            dma("sp", wst[:, s, :], w_in[kc * 128:(kc + 1) * 128, :], cx.dsem(1 + s), writes=[Rwst[s]])
            op("dve" if kc % 2 == 0 else "pool",
               lambda e, kc=kc, s=s: e.tensor_copy(out=w_bf[:, kc, :], in_=wst[:, s, :]),
               reads=[Rwst[s]], writes=[Rw[kc]])

        Rst = [R("stat%d" % i) for i in range(4)]
        Rjunk = R("junk")
        Rhb = [R("hb%d" % i) for i in range(4)]
        Rtp = [R("tp%d" % i) for i in range(2)]
        RhT = [[R("hT%d_%d" % (s, t)) for t in range(4)] for s in range(2)]
        Rpf = [R("pf%d" % i) for i in range(6)]
        Rf = [R("fstg%d" % i) for i in range(8)]
        Rt = [R("tstg%d" % i) for i in range(4)]
        Rga = [R("gastg%d" % i) for i in range(2)]
        cnt = dict(pf=0, f=0, t=0, ga=0)

        feat = []
        for h in range(4):
            feat.append((C_DQ + 128 * h, "scale", s_qT[h]))
        for h in range(4):
            feat.append((C_DK + 128 * h, "copy", s_kT[h]))
        for h in range(4):
            feat.append((C_DG + 128 * h, "silu", s_gT[h]))
        for p in range(2):
            feat.append((C_GQ + 128 * p, "scale", s_gqT[p]))
        for p in range(2):
            feat.append((C_GK + 128 * p, "copy", s_gkT[p]))
        for h in range(4):
            feat.append((C_GG + 128 * h, "silu", s_ggT[h]))
        feat.append((C_GA, "ga", s_gaT))
        tokm = [(C_DV, 512, s_v), (C_GV, 512, s_gv), (C_GK, 256, s_gk)]

        def xload1(j):
            for tt in range(4):
                g = 4 * j + tt
                xsl = g % 4
                dma("sp", xs[:, xsl, :], x[g * 128:(g + 1) * 128, :], cx.dsem(3 + xsl), writes=[Rx[xsl]])

        def normA(j, tts=(0, 1, 2, 3)):
            for tt in tts:
                g = 4 * j + tt
                xsl = g % 4
                op("act", lambda e: e.activation(out=junk[:], in_=xs[:, xsl, :], func=AF.Square,
                                                 accum_out=stat[:, xsl, 0:1]),
                   reads=[Rx[xsl]], writes=[Rjunk, Rst[xsl]])
                op("act", lambda e: e.activation(out=stat[:, xsl, 1:2], in_=stat[:, xsl, 0:1], func=AF.Ln,
                                                 scale=1.0 / D, bias=epsc[:]),
                   reads=[Rst[xsl]], writes=[Rst[xsl]])
                op("act", lambda e: e.activation(out=stat[:, xsl, 2:3], in_=stat[:, xsl, 1:2], func=AF.Exp,
                                                 scale=-0.5),
                   reads=[Rst[xsl]], writes=[Rst[xsl]])
                op("dve", lambda e: e.scalar_tensor_tensor(out=hb[:, xsl, :], in0=xs[:, xsl, :],
                                                           scalar=stat[:, xsl, 2:3], in1=nw_b[:],
                                                           op0=ALU.mult, op1=ALU.mult),
                   reads=[Rx[xsl], Rst[xsl]], writes=[Rhb[xsl]])

        def normB(j):
            sl = j % 2
            for tt in range(4):
                g = 4 * j + tt
                xsl = g % 4
                hs = g % 2
                for kc in range(8):
                    op("pe", lambda e: e.transpose(out=tp[hs][:, kc, :], in_=hb[:, xsl, kc * 128:(kc + 1) * 128],
                                                   identity=ident[:]),
                       reads=[Rhb[xsl]], writes=[Rtp[hs]], inc=(kc == 7))
                op("dve" if tt % 2 == 0 else "act",
                   (lambda e: e.tensor_copy(out=hT[:, sl, :, tt * 128:(tt + 1) * 128], in_=tp[hs][:, :, :]))
                   if tt % 2 == 0 else
                   (lambda e: e.activation(out=hT[:, sl, :, tt * 128:(tt + 1) * 128], in_=tp[hs][:, :, :],
                                           func=AF.Copy)),
                   reads=[Rtp[hs]], writes=[RhT[sl][tt]])

        normA(0)
        normB(0)
        for j in range(NB):
            sl = j % 2
            if j + 1 < NB:
                xload1(j + 1)
            for fi, (c0, kind, dst) in enumerate(feat):
                if j + 1 < NB and fi % 4 == 2 and fi // 4 < 4:
                    normA(j + 1, (fi // 4,))
                b = cnt["pf"] % 6
                cnt["pf"] += 1
                M = 16 if kind == "ga" else 128
                for kc in range(8):
                    op("pe", lambda e: e.matmul(pf[b][0:M, :], lhsT=w_bf[:, kc, c0:c0 + M], rhs=hT[:, sl, kc, :],
                                                start=(kc == 0), stop=(kc == 7)),
                       reads=[Rw[kc]] + RhT[sl], writes=[Rpf[b]], inc=(kc == 7))
                if kind == "ga":
                    s = cnt["ga"] % 2
                    cnt["ga"] += 1
                    op("dve", lambda e: e.tensor_copy(out=gastg[:, s, :], in_=pf[b][0:16, :]),
                       reads=[Rpf[b]], writes=[Rga[s]])
                    dma("sp", dst[:, j * 512:(j + 1) * 512], gastg[:, s, :], cx.dsem(7 + s), reads=[Rga[s]])
                    continue
                s = cnt["f"] % 8
                cnt["f"] += 1
                if kind == "scale":
                    op("dve", lambda e: e.tensor_scalar(out=fstg[:, s, :], in0=pf[b][:, :], scalar1=0.125,
                                                        scalar2=None, op0=ALU.mult),
                       reads=[Rpf[b]], writes=[Rf[s]])
                elif kind == "copy":
                    op("dve", lambda e: e.tensor_copy(out=fstg[:, s, :], in_=pf[b][:, :]),
                       reads=[Rpf[b]], writes=[Rf[s]])
                else:
                    op("act", lambda e: e.activation(out=fstg[:, s, :], in_=pf[b][:, :], func=AF.Silu),
                       reads=[Rpf[b]], writes=[Rf[s]])
                dma("sp", dst[:, j * 512:(j + 1) * 512], fstg[:, s, :], cx.dsem(9 + s), reads=[Rf[s]])
            if j + 1 < NB:
                normB(j + 1)
            for tt in range(4):
                g = 4 * j + tt
                for (c0, n, dst) in tokm:
                    b = cnt["pf"] % 6
                    cnt["pf"] += 1
                    for kc in range(8):
                        op("pe", lambda e: e.matmul(pf[b][:, 0:n], lhsT=hT[:, sl, kc, tt * 128:(tt + 1) * 128],
                                                    rhs=w_bf[:, kc, c0:c0 + n], start=(kc == 0), stop=(kc == 7)),
                           reads=[Rw[kc], RhT[sl][tt]], writes=[Rpf[b]], inc=(kc == 7))
                    s = cnt["t"] % 4
                    cnt["t"] += 1
                    op("dve" if cnt["t"] % 3 else "act",
                       (lambda e: e.tensor_copy(out=tstg[:, s, 0:n], in_=pf[b][:, 0:n])) if cnt["t"] % 3 else
                       (lambda e: e.activation(out=tstg[:, s, 0:n], in_=pf[b][:, 0:n], func=AF.Copy)),
                       reads=[Rpf[b]], writes=[Rt[s]])
                    dma("sp", dst[g * 128:(g + 1) * 128, :], tstg[:, s, 0:n], cx.dsem(17 + s), reads=[Rt[s]])
        cx.barrier()

    if dbg == 1:
        gst.close()
        cx.stack.close()
        return nc

    with ExitStack() as st:
        KT = sb(st, "KT", [128, 2, T], BF16)
        QT = sb(st, "QT", [128, 2, T], BF16)
        GT = sb(st, "GT", [128, 2, T], BF16)
        V = sb(st, "V", [128, 2, NT, 128], BF16)
        Pt = sb(st, "Pt", [128, 3, 2, 512], BF16)
        osb2 = sb(st, "osb2", [128, 2, 512], F32)
        zsb = sb(st, "zsb", [128, 2, 512], F32)
        ob = sb(st, "ob", [128, 4, 512], F32)
        sqb = sb(st, "sqb", [128, 4, 512], BF16)
        lnb = sb(st, "lnb", [128, 4, 512], F32)
        rsb = sb(st, "rsb", [128, 4, 512], F32)
        tmpb = sb(st, "tmpb", [128, 512], F32)
        ystg = sb(st, "ystg", [128, 4, 512], BF16)
        Sall = pst(st, "Sall", [128, 4, 512], F32)
        accO = pst(st, "accO", [128, 2, 512], F32)
        accZ = pst(st, "accZ", [128, 2, 512], F32)
        acc = [accO[:, 0, :], accO[:, 1, :], accZ[:, 0, :], accZ[:, 1, :]]

        RK = [R("K%d" % i) for i in range(2)]
        RQ = [R("Q%d" % i) for i in range(2)]
        RG = [R("G%d" % i) for i in range(2)]
        RV = [R("V%d" % i) for i in range(2)]
        RS = [R("S%d" % i) for i in range(2)]
        RP = [R("P%d" % i) for i in range(3)]
        RA = [R("acc%d" % i) for i in range(4)]
        Ros2 = [R("osb2_%d" % i) for i in range(2)]
        Rzs = [R("zsb%d" % i) for i in range(2)]
        Rob = [R("ob%d" % i) for i in range(4)]
        Rsq = [R("sq%d" % i) for i in range(4)]
        Rln = [R("ln%d" % i) for i in range(4)]
        Rrs = [R("rs%d" % i) for i in range(4)]
        Rtmp = R("tmp")
        Rys = [R("ys%d" % i) for i in range(4)]

        def load_head(h):
            hs = h % 2
            CH = min(T, 2048)
            for c in range(0, T, CH):
                dma("sp", KT[:, hs, c:c + CH], s_kT[h][:, c:c + CH], cx.dsem(21 + hs), writes=[RK[hs]])
                dma("sp", QT[:, hs, c:c + CH], s_qT[h][:, c:c + CH], cx.dsem(23 + hs), writes=[RQ[hs]])
            for t0 in range(0, NT, 8):
                dma("sp", V[:, hs, t0:t0 + 8, :],
                    s_v[t0 * 128:(t0 + 8) * 128, h * 128:(h + 1) * 128].rearrange("(t p) e -> p t e", p=128),
                    cx.dsem(25 + hs), writes=[RV[hs]])
            for c in range(0, T, CH):
                dma("sp", GT[:, hs, c:c + CH], s_gT[h][:, c:c + CH], cx.dsem(27 + hs), writes=[RG[hs]])

        pairs = []
        qbc = 0
        for h in range(4):
            Nq = 256 if h == 0 else 512
            r = Nq // 128
            for qb in range(T // Nq):
                nk = (qb + 1) * r
                kt_lo = 0
                if WIN_THR is not None:
                    while kt_lo < nk - r and SLOPES[h] * (qb * Nq - (kt_lo * 128 + 127)) >= WIN_THR:
                        kt_lo += 1
                for kt in range(kt_lo, nk):
                    pairs.append((h, qb, kt, nk, Nq, r, qbc, kt_lo))
                qbc += 1

        pending = []
        sc = [0]

        def defer(at, fn):
            pending.append((at, len(pending), fn))
            pending.sort(key=lambda t: (t[0], t[1]))

        def epilogue(n, h, qb, Nq, qbc):
            hs = h % 2
            q0 = qb * Nq
            osl = qbc % 4
            op("dve", lambda e: e.tensor_copy(out=osb2[:, :, :Nq], in_=accO[:, :, :Nq]), reads=[RA[0], RA[1]],
               writes=[Ros2[0], Ros2[1]])
            op("act", lambda e: e.activation(out=zsb[:, :, :Nq], in_=accZ[:, :, :Nq], func=AF.Copy),
               reads=[RA[2], RA[3]], writes=[Rzs[0], Rzs[1]])
            for i in range(2):
                op("dve", lambda e: e.reciprocal(out=zsb[:, i, :Nq], in_=zsb[:, i, :Nq]), reads=[Rzs[i]], writes=[Rzs[i]])
                op("dve", lambda e: e.tensor_tensor(out=osb2[:, i, :Nq], in0=osb2[:, i, :Nq], in1=zsb[:, i, :Nq],
                                                    op=ALU.mult), reads=[Ros2[i], Rzs[i]], writes=[Ros2[i]])
            op("dve", lambda e: e.scalar_tensor_tensor(out=ob[:, osl, :Nq], in0=osb2[:, 1, :Nq], scalar=neglam[:],
                                                       in1=osb2[:, 0, :Nq], op0=ALU.mult, op1=ALU.add),
               reads=[Ros2[0], Ros2[1], Rl], writes=[Rob[osl]])
            op("pool", lambda e: e.tensor_tensor(out=sqb[:, osl, :Nq], in0=ob[:, osl, :Nq], in1=ob[:, osl, :Nq],
                                                 op=ALU.mult), reads=[Rob[osl]], writes=[Rsq[osl]])
            box = {}

            def f_pe():
                s = sc[0] % 2
                sc[0] += 1
                box["s"] = s
                op("pe", lambda e: e.matmul(Sall[:, 2 * s, :Nq], lhsT=onesmb[:], rhs=sqb[:, osl, :Nq], start=True,
                                            stop=True), reads=[Rsq[osl], Rc], writes=[RS[s]])

            def f_act():
                s = box["s"]
                op("act", lambda e: e.activation(out=lnb[:, osl, :Nq], in_=Sall[:, 2 * s, :Nq], func=AF.Ln,
                                                 bias=epsc[:], scale=1.0), reads=[RS[s]], writes=[Rln[osl]])
                op("act", lambda e: e.activation(out=rsb[:, osl, :Nq], in_=lnb[:, osl, :Nq], func=AF.Exp, scale=-0.5),
                   reads=[Rln[osl]], writes=[Rrs[osl]])

            def f_dve():
                op("dve", lambda e: e.tensor_tensor(out=tmpb[:, :Nq], in0=ob[:, osl, :Nq], in1=rsb[:, osl, :Nq],
                                                    op=ALU.mult), reads=[Rob[osl], Rrs[osl]], writes=[Rtmp])
                op("dve", lambda e: e.scalar_tensor_tensor(out=ystg[:, osl, :Nq], in0=tmpb[:, :Nq], scalar=wcol_d[:],
                                                           in1=GT[:, hs, q0:q0 + Nq], op0=ALU.mult, op1=ALU.mult),
                   reads=[Rtmp, RG[hs], Rl], writes=[Rys[osl]])
                dma("sp", s_yT[h][:, q0:q0 + Nq], ystg[:, osl, :Nq], cx.dsem(52 + osl), reads=[Rys[osl]])

            defer(n + 13, f_pe)
            defer(n + 13, f_act)
            defer(n + 15, f_dve)

        LA = 1
        NPR = len(pairs)
        load_head(0)
        for n in range(NPR + LA):
            while pending and pending[0][0] <= n:
                pending.pop(0)[2]()
            if n < NPR:
                h, qb, kt, nk, Nq, r, qbc, kt_lo = pairs[n]
                hs = h % 2
                if qb == 0 and kt == kt_lo and h + 1 < 4:
                    cnt_h = sum(1 for p in pairs if p[0] == h)

                    def ld(h=h):
                        while pending:
                            pending.pop(0)[2]()
                        load_head(h + 1)
                    defer(n + min(20, max(1, cnt_h - 2)), ld)
                q0 = qb * Nq
                s = sc[0] % 2
                sc[0] += 1
                sP = n % 3
                jd = kt - r * qb
                cl = 128 * jd if jd > 0 else 0
                diag = jd >= 0
                for i in range(2):
                    op("pe", lambda e: e.matmul(Sall[:, 2 * s + i, cl:Nq],
                                                lhsT=KT[64 * i:64 * i + 64, hs, kt * 128:(kt + 1) * 128],
                                                rhs=QT[64 * i:64 * i + 64, hs, q0 + cl:q0 + Nq], start=True,
                                                stop=not diag),
                       reads=[RK[hs], RQ[hs]], writes=[RS[s]], inc=(i == 1 and not diag))
                if diag:
                    for i in range(2):
                        op("pe", lambda e: e.matmul(Sall[:, 2 * s + i, cl:cl + 128], lhsT=ident[:], rhs=Mtri[:],
                                                    start=False, stop=True),
                           reads=[Rc], writes=[RS[s]], inc=(i == 1))
                jj = r * qb - kt + 3
                op("act", lambda e: e.activation(out=Pt[:, sP, :, cl:Nq], in_=Sall[:, 2 * s:2 * s + 2, cl:Nq],
                                                 func=AF.Exp, bias=bts[h][:, jj:jj + 1], scale=1.0),
                   reads=[RS[s], Rc], writes=[RP[sP]])
            m = n - LA
            if m >= 0:
                h, qb, kt, nk, Nq, r, qbc, kt_lo = pairs[m]
                hs = h % 2
                sP = m % 3
                jd = kt - r * qb
                cl = 128 * jd if jd > 0 else 0
                for i in range(2):
                    op("pe", lambda e: e.matmul(acc[i][:, cl:Nq], lhsT=V[:, hs, kt, :], rhs=Pt[:, sP, i, cl:Nq],
                                                start=(kt == kt_lo), stop=(kt == nk - 1)),
                       reads=[RV[hs], RP[sP]], writes=[RA[i]], inc=False)
                for i in range(2):
                    op("pe", lambda e: e.matmul(acc[2 + i][:, cl:Nq], lhsT=onesb[:], rhs=Pt[:, sP, i, cl:Nq],
                                                start=(kt == kt_lo), stop=(kt == nk - 1)),
                       reads=[RP[sP], Rc], writes=[RA[2 + i]], inc=(i == 1))
                if kt == nk - 1:
                    epilogue(n, h, qb, Nq, qbc)
        while pending:
            pending.pop(0)[2]()
        cx.barrier()

    if dbg == 2:
        gst.close()
        cx.stack.close()
        return nc

    if not FUSE4:
        wo_bf = sb(gst, "wo_bf", [128, 8, D], BF16)
    with ExitStack() as st:
        gqTb = sb(st, "gqTb", [128, 3, 2, 512], BF16)
        gkTb = sb(st, "gkTb", [128, 3, 2, 512], BF16)
        ggTb = sb(st, "ggTb", [128, 3, 4, 512], BF16)
        gvb = sb(st, "gvb", [128, 3, 4, 512], BF16)
        gkb = sb(st, "gkb", [128, 3, 4, 256], BF16)
        gab = sb(st, "gab", [17, 3, 512], BF16)
        wup = sb(st, "wup", [17, 256], BF16)
        wupf = sb(st, "wupf", [17, 256], F32)
        gmask4 = sb(st, "gmask4", [128, 4, 128], F32)
        e1b = sb(st, "e1b", [128, 256], F32)
        ltb = sb(st, "ltb", [128, 2, 256], F32)
        lhi = sb(st, "lhi", [128, 2, 256], BF16)
        llo = sb(st, "llo", [128, 2, 256], BF16)
        ebT = sb(st, "ebT", [128, 2, 2, 128], F32)
        enbT = sb(st, "enbT", [128, 2, 128], F32)
        ksd = sb(st, "ksd", [128, 256], F32)
        qz = sb(st, "qz", [128, 2, 4, 128], BF16)
        kin = sb(st, "kin", [128, 2, 2, 128], BF16)
        kst = sb(st, "kst", [128, 2, 256], BF16)
        aTb = sb(st, "aTb", [128, 4, 128], BF16)
        Sf = sb(st, "Sf", [128, 2, 128], F32)
        Sbf = sb(st, "Sbf", [128, 2, 2, 128], BF16)
        osb = sb(st, "osb", [128, 4, 512], F32)
        sqg = sb(st, "sqg", [128, 2, 512], BF16)
        lng = sb(st, "lng", [128, 512], F32)
        rsg = sb(st, "rsg", [128, 2, 512], F32)
        tmpg = sb(st, "tmpg", [128, 512], F32)
        yg = sb(st, "yg", [128, 8, 4, 128], BF16)
        pA_ = pst(st, "pA", [128, 512], F32)
        pB_ = pst(st, "pB", [128, 512], F32)
        pC_ = pst(st, "pC", [128, 512], F32)
        pD_ = pst(st, "pD", [128, 512], F32)
        pE_ = pst(st, "pE", [128, 512], F32)
        pF_ = pst(st, "pF", [128, 512], F32)
        pG = pst(st, "pG", [128, 512], F32)
        pA = pA_[:, 0:256]
        pB = pB_[:, 0:256].rearrange("p (a b) -> p a b", a=2)
        pC = pC_[:, 0:256]
        pD = pD_[:, :].rearrange("p (a b) -> p a b", a=4)
        pE = pE_[:, :].rearrange("p (a b) -> p a b", a=4)
        pF = pF_[:, :].rearrange("p (a b) -> p a b", a=2)

        Rblk = [[R("gblk%d_%d" % (k, s)) for s in range(3)] for k in range(6)]
        Rg0 = R("g0")
        RA, RB, RC, RD, RE, RF, RG_ = [R("pg%d" % i) for i in range(7)]
        Re1 = R("e1")
        Rlt = [R("lt%d" % i) for i in range(2)]
        Rlh = [R("lh%d" % i) for i in range(2)]
        Rll = [R("ll%d" % i) for i in range(2)]
        Reb = [R("eb%d" % i) for i in range(2)]
        Renb = R("enb")
        Rksd = R("ksd")
        Rqin = [R("qin%d" % i) for i in range(2)]
        Rkin = [R("kin%d" % i) for i in range(2)]
        Rkst = [R("kst%d" % i) for i in range(2)]
        RaT = R("aT")
        RSf = R("Sf")
        RSf4 = [R("Sf%d" % i) for i in range(4)]
        RSb = [R("Sb%d" % i) for i in range(2)]
        Ros = [R("os%d" % i) for i in range(4)]
        Rsqg = [R("sqg%d" % i) for i in range(2)]
        Rlng = R("lng")
        Rrsg = [R("rsg%d" % i) for i in range(2)]
        Rtmpg = R("tmpg")
        Ryg = [R("yg%d" % i) for i in range(8)]

        op("pool", lambda e: e.memset(gab[:], 1.0), writes=[Rblk[5][0], Rblk[5][1], Rblk[5][2]])
        op("pool", lambda e: e.memset(Sf[:], 0.0), writes=[RSf])
        op("pool", lambda e: e.memset(Sbf[:], 0.0), writes=[RSb[0], RSb[1]])
        op("pool", lambda e: e.memset(qz[:], 0.0), writes=[Rqin[0], Rqin[1]])
        for hh in range(4):
            op("pool", lambda e: e.tensor_copy(out=gmask4[:, hh, :], in_=gmask[:]), reads=[Rc], writes=[Rg0])
        Rwupf = R("wupf")
        dma("sp", wupf[0:16, :], w_gate_up, cx.dsem(0), writes=[Rwupf])
        dma("sp", wupf[16:17, :], b_gate, cx.dsem(0), writes=[Rwupf])
        op("pool", lambda e: e.tensor_copy(out=wup[:], in_=wupf[:]), reads=[Rwupf], writes=[Rg0])

        def gload(j):
            s = j % 3
            c = slice(j * 512, (j + 1) * 512)
            dma("sp", gab[0:16, s, :], s_gaT[:, c], cx.dsem(31 + s), writes=[Rblk[5][s]])
            dma("sp", gqTb[:, s, :, :], s_gqT[:, :, c].rearrange("c p t -> p c t"), cx.dsem(34 + s), writes=[Rblk[0][s]])
            dma("sp", gkTb[:, s, :, :], s_gkT[:, :, c].rearrange("c p t -> p c t"), cx.dsem(37 + s), writes=[Rblk[1][s]])
            dma("sp", gkb[:, s, :, :], s_gk[c, :].rearrange("(t p) e -> p t e", p=128), cx.dsem(40 + s),
                writes=[Rblk[4][s]])
            dma("sp", gvb[:, s, :, :], s_gv[c, :].rearrange("(t p) e -> p t e", p=128), cx.dsem(43 + s),
                writes=[Rblk[3][s]])
            dma("sp", ggTb[:, s, :, :], s_ggT[:, :, c].rearrange("c p t -> p c t"), cx.dsem(46 + s), writes=[Rblk[2][s]])

        def s1a(t):
            j, tt = divmod(t, 4)
            s = j % 3
            c0 = tt * 128
            ls = t % 2
            op("pe", lambda e: e.matmul(pA[:, :], lhsT=gab[0:17, s, c0:c0 + 128], rhs=wup[0:17, :], start=True, stop=True),
               reads=[Rblk[5][s], Rg0], writes=[RA])
            op("act", lambda e: e.activation(out=e1b[:], in_=pA[:, :], func=AF.Exp, scale=-1.0), reads=[RA], writes=[Re1])
            op("act", lambda e: e.activation(out=ltb[:, ls, :], in_=e1b[:], func=AF.Ln, bias=onesf[:, 0:1], scale=1.0),
               reads=[Re1, Rc], writes=[Rlt[ls]])
            op("dve", lambda e: e.tensor_copy(out=lhi[:, ls, :], in_=ltb[:, ls, :]), reads=[Rlt[ls]], writes=[Rlh[ls]])
            op("pool", lambda e: e.tensor_tensor(out=llo[:, ls, :], in0=ltb[:, ls, :], in1=lhi[:, ls, :],
                                                 op=ALU.subtract), reads=[Rlt[ls], Rlh[ls]], writes=[Rll[ls]])

        def s1b(t):
            j, tt = divmod(t, 4)
            s = j % 3
            c0 = tt * 128
            ls = t % 2
            for p in range(2):
                op("pe", lambda e: e.matmul(pB[:, p, :], lhsT=lhi[:, ls, p * 128:(p + 1) * 128], rhs=Ucsb[:],
                                            start=True, stop=False), reads=[Rlh[ls], Rc], writes=[RB], inc=False)
                op("pe", lambda e: e.matmul(pB[:, p, :], lhsT=llo[:, ls, p * 128:(p + 1) * 128], rhs=Ucsb[:],
                                            start=False, stop=True), reads=[Rll[ls], Rc], writes=[RB])
            op("pe", lambda e: e.matmul(pC[:, :], lhsT=Ustb[:], rhs=lhi[:, ls, :], start=True, stop=False),
               reads=[Rlh[ls], Rc], writes=[RC], inc=False)
            op("pe", lambda e: e.matmul(pC[:, :], lhsT=Ustb[:], rhs=llo[:, ls, :], start=False, stop=True),
               reads=[Rll[ls], Rc], writes=[RC])
            op("act", lambda e: e.activation(out=ebT[:, ls, :, :], in_=pB[:, :, :], func=AF.Exp), reads=[RB], writes=[Reb[ls]])
            op("act", lambda e: e.activation(out=enbT[:, :, :], in_=pB[:, :, :], func=AF.Exp, scale=-1.0),
               reads=[RB], writes=[Renb])
            op("act", lambda e: e.activation(out=ksd[:], in_=pC[:, :], func=AF.Exp), reads=[RC], writes=[Rksd])
            for h2 in range(2):
                r0 = 64 * h2
                op("dve", lambda e: e.tensor_tensor(out=qz[r0:r0 + 64, ls, h2::2, :],
                                                    in0=gqTb[r0:r0 + 64, s, :, c0:c0 + 128],
                                                    in1=ebT[r0:r0 + 64, ls, :, :], op=ALU.mult),
                   reads=[Rblk[0][s], Reb[ls]], writes=[Rqin[ls]])
            op("pool", lambda e: e.tensor_tensor(out=kin[:, ls, :, :], in0=gkTb[:, s, :, c0:c0 + 128], in1=enbT[:, :, :],
                                                 op=ALU.mult), reads=[Rblk[1][s], Renb], writes=[Rkin[ls]])
            op("pool", lambda e: e.tensor_tensor(out=kst[:, ls, :], in0=gkb[:, s, tt, :], in1=ksd[:], op=ALU.mult),
               reads=[Rblk[4][s], Rksd], writes=[Rkst[ls]])

        def s2a(t):
            j, tt = divmod(t, 4)
            s = j % 3
            ls = t % 2
            cur, nxt = t % 2, (t + 1) % 2
            for hh in range(4):
                p, r0 = hh // 2, 64 * (hh % 2)
                op("pe", lambda e: e.matmul(pD[:, hh, :], lhsT=kin[:, ls, p, :], rhs=qz[:, ls, hh, :],
                                            start=True, stop=True), reads=[Rkin[ls], Rqin[ls]], writes=[RD])
            for p in range(2):
                op("pe", lambda e: e.matmul(pF[:, p, :], lhsT=kst[:, ls, p * 128:(p + 1) * 128],
                                            rhs=gvb[:, s, tt, p * 256:(p + 1) * 256], start=True, stop=True),
                   reads=[Rkst[ls], Rblk[3][s]], writes=[RF])
            op("dve", lambda e: e.tensor_tensor(out=aTb[:, :, :], in0=pD[:, :, :], in1=gmask4[:, :, :], op=ALU.mult),
               reads=[RD, Rg0], writes=[RaT])
            for hh in range(4):
                p, r0 = hh // 2, 64 * (hh % 2)
                op("pe", lambda e: e.matmul(pE[:, hh, :], lhsT=gvb[:, s, tt, hh * 128:(hh + 1) * 128], rhs=aTb[:, hh, :],
                                            start=True, stop=False), reads=[Rblk[3][s], RaT], writes=[RE])
                op("pe", lambda e: e.matmul(pE[:, hh, :], lhsT=Sbf[:, cur, p, :], rhs=qz[:, ls, hh, :],
                                            start=False, stop=True), reads=[RSb[cur], Rqin[ls]], writes=[RE])
            for hh in range(4):
                p, h2 = hh // 2, hh % 2
                r0 = 64 * h2
                op("dve", lambda e: e.scalar_tensor_tensor(out=Sf[r0:r0 + 64, p, :], in0=Sf[r0:r0 + 64, p, :],
                                                           scalar=ebT[r0:r0 + 64, ls, p, 127:128],
                                                           in1=pF[r0:r0 + 64, p, h2 * 128:(h2 + 1) * 128],
                                                           op0=ALU.mult, op1=ALU.add),
                   reads=[RSf, RSf4[hh], Reb[ls], RF], writes=[RSf4[hh]])
            op("pool", lambda e: e.tensor_copy(out=Sbf[:, nxt, :, :], in_=Sf[:, :, :]), reads=[RSf] + RSf4,
               writes=[RSb[nxt]])
            o4 = t % 4
            op("act", lambda e: e.activation(out=osb[:, o4, :], in_=pE_[:, :], func=AF.Copy), reads=[RE], writes=[Ros[o4]])

        def s2b(t):
            o4, s2 = t % 4, t % 2
            op("pool", lambda e: e.tensor_tensor(out=sqg[:, s2, :], in0=osb[:, o4, :], in1=osb[:, o4, :], op=ALU.mult),
               reads=[Ros[o4]], writes=[Rsqg[s2]])

        def s2c(t):
            s2 = t % 2
            op("pe", lambda e: e.matmul(pG[:, :], lhsT=onesmb[:], rhs=sqg[:, s2, :], start=True, stop=True),
               reads=[Rsqg[s2], Rc], writes=[RG_])
            op("act", lambda e: e.activation(out=lng[:], in_=pG[:, :], func=AF.Ln, bias=epsc[:], scale=1.0),
               reads=[RG_], writes=[Rlng])
            op("act", lambda e: e.activation(out=rsg[:, s2, :], in_=lng[:], func=AF.Exp, scale=-0.5),
               reads=[Rlng], writes=[Rrsg[s2]])

        def s2d(t):
            j, tt = divmod(t, 4)
            s = j % 3
            c0 = tt * 128
            o4, s2 = t % 4, t % 2
            op("dve", lambda e: e.tensor_tensor(out=tmpg[:], in0=osb[:, o4, :], in1=rsg[:, s2, :], op=ALU.mult),
               reads=[Ros[o4], Rrsg[s2]], writes=[Rtmpg])
            s8 = t % 8
            op("dve", lambda e: e.scalar_tensor_tensor(out=yg[:, s8, :, :], in0=tmpg[:], scalar=wcol_g[:],
                                                       in1=ggTb[:, s, :, c0:c0 + 128], op0=ALU.mult, op1=ALU.mult),
               reads=[Rtmpg, Rblk[2][s], Rl], writes=[Ryg[s8]])
            if not FUSE4:
                dma("sp", s_yT[4:8, :, t * 128:(t + 1) * 128].rearrange("h p t -> p h t"), yg[:, s8, :, :],
                    cx.dsem(49 + s2), reads=[Ryg[s8]])

        if not FUSE4:
            wst2g = sb(st, "wst2g", [128, 2, D], F32)
            Rwo = [R("wo%d" % k) for k in range(8)]
            Rwst2 = [R("wst2_%d" % k) for k in range(2)]

            def wchunk(kc):
                s = kc % 2
                dma("sp", wst2g[:, s, :], w_out[kc * 128:(kc + 1) * 128, :], cx.dsem(1 + s), writes=[Rwst2[s]])
                op("pool", lambda e: e.tensor_copy(out=wo_bf[:, kc, :], in_=wst2g[:, s, :]), reads=[Rwst2[s]],
                   writes=[Rwo[kc]])
        if FUSE4:
            wo_bf = sb(st, "wo_bf", [128, 8, D], BF16)
            wst2 = sb(st, "wst2", [128, 2, D], F32)
            yTb = sb(st, "yTb", [128, 2, 4, 512], BF16)
            xs4 = sb(st, "xs4", [128, 4, D], F32)
            zb = sb(st, "zb", [128, 2, D], F32)
            junk4 = sb(st, "junk4", [128, D], F32)
            stat4 = sb(st, "stat4", [128, 2, 4], F32)
            ost = sb(st, "ost", [128, 2, D], F32)
            po1 = pst(st, "po1", [128, 512], F32)
            Rwo = [R("wo%d" % k) for k in range(8)]
            Rwst2 = [R("wst2_%d" % k) for k in range(2)]
            for kc in range(8):
                s = kc % 2
                dma("sp", wst2[:, s, :], w_out[kc * 128:(kc + 1) * 128, :], cx.dsem(1 + s), writes=[Rwst2[s]])
                op("dve" if kc % 2 == 0 else "pool",
                   lambda e: e.tensor_copy(out=wo_bf[:, kc, :], in_=wst2[:, s, :]), reads=[Rwst2[s]], writes=[Rwo[kc]])
            RyT = [R("yT%d" % i) for i in range(2)]
            Rx4 = [R("x4_%d" % i) for i in range(4)]
            Rpo1 = R("po1")
            Rz = [R("z%d" % i) for i in range(2)]
            Rj4 = R("junk4")
            Rs4 = [R("s4_%d" % i) for i in range(2)]
            Ro = [R("ost%d" % i) for i in range(2)]

            def yload(j):
                s = j % 2
                dma("sp", yTb[:, s, :, :], s_yT[0:4, :, j * 512:(j + 1) * 512].rearrange("c p t -> p c t"),
                    cx.dsem(21 + s), writes=[RyT[s]])

            def xload4(g):
                xsl = g % 4
                dma("sp", xs4[:, xsl, :], x[g * 128:(g + 1) * 128, :], cx.dsem(3 + xsl), writes=[Rx4[xsl]])

            def p4tile(g):
                j, tt = divmod(g, 4)
                sl = j % 2
                xsl = g % 4
                z2 = g % 2
                s8 = g % 8
                if tt == 0 and j + 1 < NB:
                    yload(j + 1)
                if g + 2 < NT:
                    xload4(g + 2)
                if P4L < 2:
                    return
                for nh in range(2):
                    for kc in range(8):
                        if kc < 4:
                            lh = yTb[:, sl, kc, tt * 128:(tt + 1) * 128]
                            rd = [RyT[sl], Rwo[kc]]
                        else:
                            lh = yg[:, s8, kc - 4, :]
                            rd = [Ryg[s8], Rwo[kc]]
                        op("pe", lambda e: e.matmul(po1[:, :], lhsT=lh, rhs=wo_bf[:, kc, nh * 512:(nh + 1) * 512],
                                                    start=(kc == 0), stop=(kc == 7)), reads=rd, writes=[Rpo1],
                           inc=(kc == 7))
                    if P4L < 3:
                        continue
                    op("dve", lambda e: e.tensor_tensor(out=zb[:, z2, nh * 512:(nh + 1) * 512], in0=po1[:, :],
                                                        in1=xs4[:, xsl, nh * 512:(nh + 1) * 512], op=ALU.add),
                       reads=[Rpo1, Rx4[xsl]], writes=[Rz[z2]])
                if P4L < 4:
                    return
                op("act", lambda e: e.activation(out=junk4[:], in_=zb[:, z2, :], func=AF.Square,
                                                 accum_out=stat4[:, z2, 0:1]), reads=[Rz[z2]], writes=[Rj4, Rs4[z2]])
                op("act", lambda e: e.activation(out=stat4[:, z2, 1:2], in_=stat4[:, z2, 0:1], func=AF.Ln, scale=1.0 / D,
                                                 bias=epsc[:]), reads=[Rs4[z2]], writes=[Rs4[z2]])
                op("act", lambda e: e.activation(out=stat4[:, z2, 2:3], in_=stat4[:, z2, 1:2], func=AF.Exp, scale=-0.5),
                   reads=[Rs4[z2]], writes=[Rs4[z2]])
                if P4L < 5:
                    return
                op("dve", lambda e: e.scalar_tensor_tensor(out=ost[:, z2, :], in0=zb[:, z2, :], scalar=stat4[:, z2, 2:3],
                                                           in1=fnw_b[:], op0=ALU.mult, op1=ALU.mult),
                   reads=[Rz[z2], Rs4[z2], Rl], writes=[Ro[z2]])
                if P4L < 6:
                    return
                dma("sp", out[g * 128:(g + 1) * 128, :], ost[:, z2, :], cx.dsem(56 + z2), reads=[Ro[z2]])

            yload(0)
            xload4(0)
            xload4(1)

        gload(0)
        for it in range(NT + (7 if FUSE4 else 5)):
            if it % 4 == 1 and (it // 4) + 1 < NB:
                gload(it // 4 + 1)
            if not FUSE4 and 2 <= it < 10:
                wchunk(it - 2)
            if it < NT and GST >= 1:
                s1a(it)
            if 0 <= it - 1 < NT and GST >= 2:
                s1b(it - 1)
            if 0 <= it - 2 < NT and GST >= 3:
                s2a(it - 2)
            if 0 <= it - 3 < NT and GST >= 4:
                s2b(it - 3)
            if 0 <= it - 4 < NT and GST >= 5:
                s2c(it - 4)
            if 0 <= it - 5 < NT and GST >= 6:
                s2d(it - 5)
            if FUSE4 and 0 <= it - 6 < NT and P4L >= 1:
                p4tile(it - 6)
        cx.barrier()

    if dbg == 3:
        gst.close()
        cx.stack.close()
        return nc

    with ExitStack() as st:
        if FUSE4:
            gst.close()
            cx.stack.close()
            return nc
        yTb = sb(st, "yTb", [128, 2, 8, 512], BF16)
        xs4 = sb(st, "xs4", [128, 4, D], F32)
        zb = sb(st, "zb", [128, 2, D], F32)
        junk4 = sb(st, "junk4", [128, D], F32)
        stat4 = sb(st, "stat4", [128, 2, 4], F32)
        ost = sb(st, "ost", [128, 2, D], F32)
        po = [pst(st, "po%d" % i, [128, 512], F32) for i in range(4)]

        RyT = [R("yT%d" % i) for i in range(2)]
        Rx4 = [R("x4_%d" % i) for i in range(4)]
        Rpo = [R("po%d" % i) for i in range(4)]
        Rz = [R("z%d" % i) for i in range(2)]
        Rj4 = R("junk4")
        Rs4 = [R("s4_%d" % i) for i in range(2)]
        Ro = [R("ost%d" % i) for i in range(2)]

        def yload(j):
            s = j % 2
            dma("sp", yTb[:, s, :, :], s_yT[:, :, j * 512:(j + 1) * 512].rearrange("c p t -> p c t"), cx.dsem(21 + s),
                writes=[RyT[s]])

        def xload4(g):
            xsl = g % 4
            dma("sp", xs4[:, xsl, :], x[g * 128:(g + 1) * 128, :], cx.dsem(3 + xsl), writes=[Rx4[xsl]])

        fin4 = [None]
        yload(0)
        xload4(0)
        xload4(1)
        for j in range(NB):
            sl = j % 2
            if j + 1 < NB:
                yload(j + 1)
            for tt in range(4):
                g = 4 * j + tt
                xsl = g % 4
                z2 = g % 2
                if g + 2 < NT:
                    xload4(g + 2)
                for nh in range(2):
                    b = 2 * z2 + nh
                    for kc in range(8):
                        op("pe", lambda e: e.matmul(po[b][:, :], lhsT=yTb[:, sl, kc, tt * 128:(tt + 1) * 128],
                                                    rhs=wo_bf[:, kc, nh * 512:(nh + 1) * 512], start=(kc == 0),
                                                    stop=(kc == 7)), reads=[RyT[sl], Rwo[kc]], writes=[Rpo[b]],
                           inc=(kc == 7))
                    op("dve", lambda e: e.tensor_tensor(out=zb[:, z2, nh * 512:(nh + 1) * 512], in0=po[b][:, :],
                                                        in1=xs4[:, xsl, nh * 512:(nh + 1) * 512], op=ALU.add),
                       reads=[Rpo[b], Rx4[xsl]], writes=[Rz[z2]])
                if fin4[0] is not None:
                    fin4[0]()
                op("act", lambda e: e.activation(out=junk4[:], in_=zb[:, z2, :], func=AF.Square,
                                                 accum_out=stat4[:, z2, 0:1]), reads=[Rz[z2]], writes=[Rj4, Rs4[z2]])
                op("act", lambda e: e.activation(out=stat4[:, z2, 1:2], in_=stat4[:, z2, 0:1], func=AF.Ln, scale=1.0 / D,
                                                 bias=epsc[:]), reads=[Rs4[z2]], writes=[Rs4[z2]])
                op("act", lambda e: e.activation(out=stat4[:, z2, 2:3], in_=stat4[:, z2, 1:2], func=AF.Exp, scale=-0.5),
                   reads=[Rs4[z2]], writes=[Rs4[z2]])

                def fin(g=g, z2=z2):
                    op("dve", lambda e: e.scalar_tensor_tensor(out=ost[:, z2, :], in0=zb[:, z2, :],
                                                               scalar=stat4[:, z2, 2:3], in1=fnw_b[:], op0=ALU.mult,
                                                               op1=ALU.mult),
                       reads=[Rz[z2], Rs4[z2], Rl], writes=[Ro[z2]])
                    dma("sp", out[g * 128:(g + 1) * 128, :], ost[:, z2, :], cx.dsem(7 + z2), reads=[Ro[z2]])
                fin4[0] = fin
        fin4[0]()
        cx.barrier()

    gst.close()
    cx.stack.close()
    return nc


_NC_CACHE = {}


def kernel(**inputs):
    xin = np.asarray(inputs["x"], dtype=np.float32)
    B, T, _ = xin.shape
    if T not in _NC_CACHE:
        _NC_CACHE[T] = build(T)
    nc = _NC_CACHE[T]

    def a2(name, shape):
        return np.ascontiguousarray(np.asarray(inputs[name], dtype=np.float32).reshape(shape))

    shared = dict(
        norm_w=a2("norm_w", (1, D)), w_in=a2("w_in", (D, INC)), w_gate_up=a2("w_gate_up", (16, 256)),
        b_gate=a2("b_gate", (1, 256)), lambda_q1=a2("lambda_q1", (1, 64)), lambda_k1=a2("lambda_k1", (1, 64)),
        lambda_q2=a2("lambda_q2", (1, 64)), lambda_k2=a2("lambda_k2", (1, 64)),
        diff_subln_w=a2("diff_subln_w", (1, 128)), gla_norm_w=a2("gla_norm_w", (1, 128)),
        w_out=a2("w_out", (D, D)), final_norm_w=a2("final_norm_w", (1, D)))
    in_maps = []
    for b in range(B):
        m = dict(shared)
        m["x"] = np.ascontiguousarray(xin[b])
        in_maps.append(m)
    res = run_bass_kernel_spmd(nc, in_maps, core_ids=list(range(B)))
    return np.stack([np.asarray(r["out"], dtype=np.float32) for r in res.results], axis=0)
```
